# Optimizing a Trainium2 kernel written in Bass

```python
import jax, jax.numpy as jnp
from jax import lax
import numpy as np

D_MODEL = 2048
BATCH = 4
SEQ = 2048
DEPTH = 4
DEC_BATCH = 128
DEC_SEQ = 8
PAST_LEN = 16384
PAGE_SIZE = 128

N_MIXERS = 3
CHUNK = 128
A_GROUPS = 8
D_A = D_MODEL
A_GROUP_DIM = D_A // A_GROUPS
B_WIDTH = 3
C_WIDTH = 31
D_C = D_MODEL
D_FF = 4 * D_MODEL
N_LAYERS_A = (DEPTH + 2) // 3
N_LAYERS_B = (DEPTH + 1) // 3
N_LAYERS_C = DEPTH // 3
EPS = 1e-6

kernel_name = 'hybrid_gmlp_shortconv_conformer_decoder_step'


def rms_norm(x, g):
    xf = x.astype(jnp.float32)
    y = xf * lax.rsqrt(jnp.mean(xf * xf, axis=-1, keepdims=True) + EPS)
    return (y * g).astype(x.dtype)


def layer_norm(x, g, b):
    xf = x.astype(jnp.float32)
    mu = jnp.mean(xf, axis=-1, keepdims=True)
    var = jnp.mean(jnp.square(xf - mu), axis=-1, keepdims=True)
    return ((xf - mu) * lax.rsqrt(var + EPS) * g + b).astype(x.dtype)


def depthwise_conv(x_ext, w):
    return lax.conv_general_dilated(
        x_ext, w[:, None, :], window_strides=(1,), padding='VALID',
        dimension_numbers=('NWC', 'WIO', 'NWC'), feature_group_count=x_ext.shape[-1])


def chunk_mlp_mixer(h, w_in, ln_g, ln_b, w_s, b_s, w_out):
    bsz, t_len, _ = h.shape
    z = jax.nn.gelu(h @ w_in, approximate=False)
    u, v = jnp.split(z, 2, axis=-1)
    v = layer_norm(v, ln_g, ln_b)
    L = min(t_len, CHUNK)
    n_chunks = t_len // L
    causal = jnp.tril(jnp.ones((L, L), dtype=bool))
    ws = jnp.where(causal, w_s[:, :L, :L], 0)
    vc = v.reshape(bsz, n_chunks, L, A_GROUPS, A_GROUP_DIM)
    mixed = jnp.einsum('gts,bcsgd->bctgd', ws, vc) + b_s[:, :L].T[None, None, :, :, None]
    y = u * mixed.reshape(bsz, t_len, D_A)
    return y @ w_out, v


def short_conv_mixer(h, past, w_in, conv_w, w_out):
    b_gate, c_gate, hx = jnp.split(h @ w_in, 3, axis=-1)
    x_ext = jnp.concatenate([past, c_gate * hx], axis=1)
    y = b_gate * depthwise_conv(x_ext, conv_w)
    return y @ w_out, x_ext[:, -(B_WIDTH - 1):]


def conformer_conv_mixer(h, past, w_pw1, b_pw1, conv_w, conv_b, ln_g, ln_b, w_pw2, b_pw2):
    a, g = jnp.split(h @ w_pw1 + b_pw1, 2, axis=-1)
    glu = a * jax.nn.sigmoid(g)
    x_ext = jnp.concatenate([past, glu], axis=1)
    c = depthwise_conv(x_ext, conv_w) + conv_b
    c = jax.nn.silu(layer_norm(c, ln_g, ln_b))
    return c @ w_pw2 + b_pw2, x_ext[:, -(C_WIDTH - 1):]


def sq_relu_mlp(h, w_up, w_down):
    return jnp.square(jax.nn.relu(h @ w_up)) @ w_down


def _normal(k, shape, scale):
    return jax.random.normal(k, shape, jnp.float32) * scale


def setup_inputs(seed: int = 0) -> dict:
    key = jax.random.key(seed)
    ks = iter(jax.random.split(key, 32))
    D = D_MODEL
    return {
        'x_prompt': _normal(next(ks), (BATCH, SEQ, D), 1.0),
        'x_sample': _normal(next(ks), (DEC_BATCH, DEC_SEQ, D), 1.0),
        'state_b_conv': _normal(next(ks), (N_LAYERS_B, DEC_BATCH, B_WIDTH - 1, D), 1.0),
        'state_c_conv': _normal(next(ks), (N_LAYERS_C, DEC_BATCH, C_WIDTH - 1, D_C), 0.5),
        'norm_mix_g': 1.0 + _normal(next(ks), (DEPTH, D), 0.1),
        'norm_ffn_g': 1.0 + _normal(next(ks), (DEPTH, D), 0.1),
        'final_norm_g': 1.0 + _normal(next(ks), (D,), 0.1),
        'a_w_in': _normal(next(ks), (N_LAYERS_A, D, 2 * D_A), D ** -0.5),
        'a_ln_g': 1.0 + _normal(next(ks), (N_LAYERS_A, D_A), 0.1),
        'a_ln_b': _normal(next(ks), (N_LAYERS_A, D_A), 0.02),
        'a_w_s': _normal(next(ks), (N_LAYERS_A, A_GROUPS, CHUNK, CHUNK), CHUNK ** -0.5),
        'a_b_s': 1.0 + _normal(next(ks), (N_LAYERS_A, A_GROUPS, CHUNK), 0.1),
        'a_w_out': _normal(next(ks), (N_LAYERS_A, D_A, D), D_A ** -0.5),
        'b_w_in': _normal(next(ks), (N_LAYERS_B, D, 3 * D), D ** -0.5),
        'b_conv_w': _normal(next(ks), (N_LAYERS_B, B_WIDTH, D), B_WIDTH ** -0.5),
        'b_w_out': _normal(next(ks), (N_LAYERS_B, D, D), D ** -0.5),
        'c_w_pw1': _normal(next(ks), (N_LAYERS_C, D, 2 * D_C), D ** -0.5),
        'c_b_pw1': _normal(next(ks), (N_LAYERS_C, 2 * D_C), 0.02),
        'c_conv_w': _normal(next(ks), (N_LAYERS_C, C_WIDTH, D_C), C_WIDTH ** -0.5),
        'c_conv_b': _normal(next(ks), (N_LAYERS_C, D_C), 0.02),
        'c_ln_g': 1.0 + _normal(next(ks), (N_LAYERS_C, D_C), 0.1),
        'c_ln_b': _normal(next(ks), (N_LAYERS_C, D_C), 0.02),
        'c_w_pw2': _normal(next(ks), (N_LAYERS_C, D_C, D), D_C ** -0.5),
        'c_b_pw2': _normal(next(ks), (N_LAYERS_C, D), 0.02),
        'ffn_w_up': _normal(next(ks), (DEPTH, D, D_FF), D ** -0.5),
        'ffn_w_down': _normal(next(ks), (DEPTH, D_FF, D), D_FF ** -0.5),
    }


def reference(x_prompt, x_sample, state_b_conv, state_c_conv, norm_mix_g, norm_ffn_g, final_norm_g,
              a_w_in, a_ln_g, a_ln_b, a_w_s, a_b_s, a_w_out,
              b_w_in, b_conv_w, b_w_out,
              c_w_pw1, c_b_pw1, c_conv_w, c_conv_b, c_ln_g, c_ln_b, c_w_pw2, c_b_pw2,
              ffn_w_up, ffn_w_down):
    def trunk(x, past_b, past_c):
        a_v, b_buf, c_buf = [], [], []
        for i in range(DEPTH):
            j = i // N_MIXERS
            kind = i % N_MIXERS
            h = rms_norm(x, norm_mix_g[i])
            if kind == 0:
                y, v = chunk_mlp_mixer(h, a_w_in[j], a_ln_g[j], a_ln_b[j], a_w_s[j], a_b_s[j], a_w_out[j])
                a_v.append(v)
            elif kind == 1:
                y, buf = short_conv_mixer(h, past_b[j], b_w_in[j], b_conv_w[j], b_w_out[j])
                b_buf.append(buf)
            else:
                y, buf = conformer_conv_mixer(h, past_c[j], c_w_pw1[j], c_b_pw1[j], c_conv_w[j], c_conv_b[j],
                                              c_ln_g[j], c_ln_b[j], c_w_pw2[j], c_b_pw2[j])
                c_buf.append(buf)
            x = x + y
            x = x + sq_relu_mlp(rms_norm(x, norm_ffn_g[i]), ffn_w_up[i], ffn_w_down[i])
        return rms_norm(x, final_norm_g), jnp.stack(a_v), jnp.stack(b_buf), jnp.stack(c_buf)

    n_prompt = x_prompt.shape[0]
    zeros_b = jnp.zeros((N_LAYERS_B, n_prompt, B_WIDTH - 1, D_MODEL), x_prompt.dtype)
    zeros_c = jnp.zeros((N_LAYERS_C, n_prompt, C_WIDTH - 1, D_C), x_prompt.dtype)
    y_prompt, _, new_b_conv_prompt, new_c_conv_prompt = trunk(x_prompt, zeros_b, zeros_c)
    y_sample, new_a_v_sample, new_b_conv_sample, new_c_conv_sample = trunk(x_sample, state_b_conv, state_c_conv)
    return (y_prompt, y_sample, new_a_v_sample, new_b_conv_prompt, new_b_conv_sample,
            new_c_conv_prompt, new_c_conv_sample)
```

```python
import numpy as np
from contextlib import ExitStack
import concourse.bass as bass
import concourse.mybir as mybir
from concourse.bass_utils import run_bass_kernel_spmd

F32 = mybir.dt.float32
BF16 = mybir.dt.bfloat16
AF = mybir.ActivationFunctionType
ALU = mybir.AluOpType

D = 2048
KC = 16
NT = 5
TG = 640
EPS = 1e-6
NCORES = 8
NRING = 3

V_NMIX, V_NFFN, V_BCW, V_CB1, V_CCW, V_CCB, V_CLG, V_CLB = 0, 4, 8, 11, 13, 44, 45, 46
NVEC = 48


class Sched:
    def __init__(self, nc):
        self.nc = nc
        self.sems = []
        self.q = {}
        for name, h in (("pe", nc.tensor), ("act", nc.scalar), ("dve", nc.vector),
                        ("pool", nc.gpsimd), ("sp", nc.sync)):
            self.q[name] = dict(h=h, waited={}, sem=self._sem("q_" + name), cnt=0)
        self.dq = {}
        for name in ("pool", "sp"):
            self.dq[name] = dict(sems=[self._sem(f"d_{name}{i}") for i in range(8)], i=0)
        self.lastw = {}
        self.readers = {}
        self.extra = {}
        self.out_tokens = []

    def _sem(self, name):
        s = self.nc.alloc_semaphore(name)
        self.sems.append(s)
        return len(self.sems) - 1

    def _wait(self, qn, toks):
        q = self.q[qn]
        for si, val in toks:
            if q["waited"].get(si, 0) < val:
                q["h"].wait_ge(self.sems[si], val)
                q["waited"][si] = val

    def _deps(self, qn, reads, writes):
        toks = {}

        def add(t):
            if t is not None and toks.get(t[0], 0) < t[1]:
                toks[t[0]] = t[1]
        for r in reads:
            add(self.lastw.get(r))
            for t in self.extra.get(r, {}).items():
                add(t)
        for w in writes:
            add(self.lastw.get(w))
            for t in self.readers.get(w, {}).items():
                add(t)
            for t in self.extra.get(w, {}).items():
                add(t)
        if qn == "pe":
            toks.pop(self.q["pe"]["sem"], None)
        return list(toks.items())

    def _commit(self, tok, reads, writes):
        for r in reads:
            d = self.readers.setdefault(r, {})
            if d.get(tok[0], 0) < tok[1]:
                d[tok[0]] = tok[1]
        for w in writes:
            self.lastw[w] = tok
            self.readers[w] = {}

    def alias_barrier(self, keys):
        toks = {}
        for k in keys:
            t = self.lastw.get(k)
            if t is not None and toks.get(t[0], 0) < t[1]:
                toks[t[0]] = t[1]
            for si, v in self.readers.get(k, {}).items():
                if toks.get(si, 0) < v:
                    toks[si] = v
            for si, v in self.extra.get(k, {}).items():
                if toks.get(si, 0) < v:
                    toks[si] = v
        for k in keys:
            self.extra[k] = dict(toks)

    def op(self, qn, fn, reads=(), writes=()):
        self._wait(qn, self._deps(qn, reads, writes))
        ins = fn()
        q = self.q[qn]
        q["cnt"] += 1
        ins.then_inc(self.sems[q["sem"]], 1)
        self._commit((q["sem"], q["cnt"]), reads, writes)

    def dma(self, qn, out, in_, reads=(), writes=(), is_output=False, **kw):
        dq = self.dq[qn]
        i = dq["i"]
        K = len(dq["sems"])
        si = dq["sems"][i % K]
        need = 16 * (i // K)
        toks = self._deps(qn, reads, writes)
        if need > 0:
            toks.append((si, need))
        self._wait(qn, toks)
        self.q[qn]["h"].dma_start(out=out, in_=in_, **kw).then_inc(self.sems[si], 16)
        dq["i"] += 1
        tok = (si, need + 16)
        self._commit(tok, reads, writes)
        if is_output:
            self.out_tokens.append(tok)

    def finish(self):
        toks = {}
        for si, v in self.out_tokens:
            toks[si] = max(toks.get(si, 0), v)
        for name, dq in self.dq.items():
            K = len(dq["sems"])
            for j in range(min(dq["i"], K)):
                idx = dq["i"] - 1 - j
                si = dq["sems"][idx % K]
                toks[si] = max(toks.get(si, 0), 16 * (idx // K + 1))
        for name in ("pe", "act", "dve"):
            q = self.q[name]
            if q["cnt"]:
                toks[q["sem"]] = q["cnt"]
        self._wait("sp", list(toks.items()))


def build(layers=(0, 1, 2, 3), groups=(0, 1), final=True, dbg=False):
    nc = bass.Bass("TRN2", target_bir_lowering=False)

    def din(name, shape):
        return nc.dram_tensor(name, list(shape), F32, kind="ExternalInput").ap()

    def dout(name, shape):
        return nc.dram_tensor(name, list(shape), F32, kind="ExternalOutput").ap()

    xin = din("xin", [1280, D])
    sb_in = din("sb_in", [32, D])
    sc_in = din("sc_in", [480, D])
    hmask_in = din("hmask", [128, 1])
    vecs_in = din("vecs", [NVEC * 16, 128])
    consts_in = din("consts", [3, 128, 128])
    fin_g = din("final_norm_g", [1, D])
    W = {}
    kinds = sorted(set(l % 3 for l in layers))
    if 0 in kinds:
        W["a_w_in"] = din("a_w_in", [2, D, 2 * D])
        W["a_ln_g"] = din("a_ln_g", [2, D])
        W["a_ln_b"] = din("a_ln_b", [2, D])
        W["a_w_s"] = din("a_w_s", [2, 8, 128, 128])
        W["a_b_s"] = din("a_b_s", [2, 8, 128])
        W["a_ws8"] = din("a_ws8", [2, 8, 128, 8])
        W["a_bs8"] = din("a_bs8", [2, 8, 128])
        W["a_w_out"] = din("a_w_out", [2, D, D])
    if 1 in kinds:
        W["b_w_in"] = din("b_w_in", [1, D, 3 * D])
        W["b_w_out"] = din("b_w_out", [1, D, D])
    if 2 in kinds:
        W["c_w_pw1"] = din("c_w_pw1", [1, D, 2 * D])
        W["c_w_pw2"] = din("c_w_pw2", [1, D, D])
        W["c_b_pw2"] = din("c_b_pw2", [1, D])
    W["ffn_w_up"] = din("ffn_w_up", [4, D, 4 * D])
    W["ffn_w_down"] = din("ffn_w_down", [4, 4 * D, D])

    y_out = dout("y_out", [1152, D])
    av_out = dout("av_out", [2, 128, D])
    bp_out = dout("bp_out", [2, D])
    bs_out = dout("bs_out", [32, D])
    cp_out = dout("cp_out", [30, D])
    cs_out = dout("cs_out", [480, D])
    xdbg = dout("xdbg", [1280, D]) if dbg else None

    es = ExitStack()

    def sb(name, shape, dt=F32):
        return es.enter_context(nc.sbuf_tensor(name, list(shape), dt))

    xt = sb("xt", [128, NT, D])
    hT = sb("hT", [128, KC, TG], BF16)
    Y = sb("Y", [128, KC, TG], BF16)
    R = sb("R", [128, 16 * 674])
    ring = sb("ring", [128, NRING, KC, 512], BF16)
    xn = sb("xn", [128, D])
    T4 = sb("T4", [128, 4, TG])
    ident = sb("ident", [128, 128])
    maskP = sb("maskP", [128, 128])
    maskS = sb("maskS", [128, 128])
    onesF = sb("onesF", [128, 128])
    wsTb = sb("wsTb", [128, 2, 2, 128])
    wld = sb("wld", [128, 2, 128])
    wrep = sb("wrep", [128, 2, 8])
    bsb = sb("bsb", [128, 2, 2, 128])
    vfm = sb("vfm", [128, NVEC * 16])
    stat = sb("stat", [128, 2, 4, 8])
    bnst = sb("bnst", [128, NT, 4, 6])
    mv = sb("mv", [128, NT, 4])
    hmask = sb("hmask_sb", [128, 1])
    lnp = sb("lnp", [128, 1, 2, 512])
    bpast = sb("bpast", [128, KC, 2])
    cpast = sb("cpast", [128, KC, 30])
    ps = es.enter_context(nc.psum_tensor("ps", [128, 8, 512], F32))

    S = Sched(nc)
    V, A, PE = nc.vector, nc.scalar, nc.tensor

    bank_state = dict(next=0, reserved=set())

    def bank():
        while True:
            b = bank_state["next"] % 8
            bank_state["next"] += 1
            if b not in bank_state["reserved"]:
                return b

    steps = []

    def wsrc(name, idx, r0, c0):
        w = W[name]
        return w[idx, r0:r0 + D, :].rearrange("(c p) n -> p c n", p=128)[:, :, c0:c0 + 512]

    def add_w(src, fn, hold=0):
        steps.append((src, fn, hold))

    def add(fn):
        steps.append((None, lambda slot: fn(), 0))

    def run_steps():
        wl = [i for i, s in enumerate(steps) if s[0] is not None]
        slot_of = {}
        issued = 0

        def issue_upto(n):
            nonlocal issued
            while issued < min(n, len(wl)):
                i = wl[issued]
                s = issued % NRING
                slot_of[i] = s
                S.dma("pool", ring[:, s], steps[i][0], writes=[("w", s)])
                issued += 1
        pos = 0
        for i, (src, fn, hold) in enumerate(steps):
            if src is not None:
                issue_upto(pos + NRING - hold)
                fn(slot_of[i])
                pos += 1
            else:
                issue_upto(pos + NRING - 1)
                fn(None)

    def tiles_of(g, stage):
        return [1, 2, 3, 4] if (g == 0 and stage == 1) else [0, 1, 2, 3, 4]

    HALO0 = 96

    def tblocks(g, stage):
        if g == 0:
            return [(128, 512)] if stage == 1 else [(HALO0, 384 - HALO0), (384, 256)]
        return [(0, 384), (384, 256)]

    def hkeys(t0, n):
        return [("hT", t) for t in range(t0 // 128, (t0 + n + 127) // 128)]

    def ykeys(t0, n):
        return [("Y", t) for t in range(t0 // 128, (t0 + n + 127) // 128)]

    def fm_unit(slot, mi, act, akeys, t0, n):
        b = bank()

        def f():
            for k in range(KC):
                ins = PE.matmul(ps[:, b, 0:n], ring[:, slot, k, mi * 128:(mi + 1) * 128],
                                act[:, k, t0:t0 + n], start=(k == 0), stop=(k == KC - 1))
            return ins
        S.op("pe", f, reads=[("w", slot)] + akeys, writes=[("ps", b)])
        return b

    def tm_unit(slot, act, akey, t):
        b = bank()

        def f():
            for k in range(KC):
                ins = PE.matmul(ps[:, b, :], act[:, k, t * 128:(t + 1) * 128],
                                ring[:, slot, k, :], start=(k == 0), stop=(k == KC - 1))
            return ins
        S.op("pe", f, reads=[("w", slot), akey], writes=[("ps", b)])
        return b

    T4ALL = [("T4", r) for r in range(4)]
    RKEYS = ([("vq", t, q) for t in range(NT) for q in range(4)] + [("ci", m) for m in range(16)]
             + [("gl", m) for m in range(16)] + ["cipast", "glpast"])
    YKEYS = ([("Y", t) for t in range(NT)] + ["sgt", "cst", ("sgy", 0), ("sgy", 1)]
             + [("dg", i_) for i_ in range(22)])
    XNALL = [("xn", r) for r in range(3)]

    def vcol(v, c):
        return vfm[:, v * 16 + c: v * 16 + c + 1]

    def setup():
        S.dma("sp", ident[:], consts_in[0], writes=["ident"])
        S.dma("sp", maskP[:], consts_in[1], writes=["maskP"])
        S.dma("sp", maskS[:], consts_in[2], writes=["maskS"])
        S.dma("sp", hmask[:], hmask_in, writes=["hmask"])
        vld = xn[:, 0:768].rearrange("p (q f) -> p q f", q=6)
        S.dma("sp", vld, vecs_in.rearrange("(q r) f -> r q f", q=6), writes=XNALL)
        S.op("dve", lambda: V.memset(onesF[:], 1.0), writes=["onesF"])
        S.op("dve", lambda: V.memset(bpast[:], 0.0), writes=["bpast"])
        S.op("dve", lambda: V.memset(cpast[:], 0.0), writes=["cpast"])
        S.op("dve", lambda: V.memset(Y[:], 0.0), writes=[("Y", t) for t in range(NT)])
        S.op("dve", lambda: V.memset(R[:], 0.0), writes=RKEYS)
        for q in range(6):
            b = bank()
            S.op("pe", lambda: PE.transpose(ps[:, b, 0:128], vld[:, q, :], ident[:]),
                 reads=XNALL + ["ident"], writes=[("ps", b)])
            S.op("dve", lambda: V.tensor_copy(out=vfm[:, q * 128:(q + 1) * 128], in_=ps[:, b, 0:128]),
                 reads=[("ps", b)], writes=["vfm"])

    def load_group(g):
        for t in range(NT):
            r0 = (g * NT + t) * 128
            S.dma("sp", xt[:, t, :], xin[r0:r0 + 128, :], writes=[("x", t)])

    def dump_dbg(g):
        for t in range(NT):
            r0 = (g * NT + t) * 128
            S.dma("sp", xdbg[r0:r0 + 128, :], xt[:, t, :], reads=[("x", t)], is_output=True)

    norm_ctr = [0]

    def rms_batch(tl):
        r = norm_ctr[0] % 2
        norm_ctr[0] += 1
        sk = ("stat", r)
        for i, t in enumerate(tl):
            S.op("act", lambda: A.activation(out=hT[:, :, t * 128:(t + 1) * 128],
                                             in_=xt[:, t, :].rearrange("p (c n) -> p c n", c=KC),
                                             func=AF.Square, accum_out=stat[:, r, 0, i:i + 1]),
                 reads=[("x", t)], writes=[("hT", t), sk])
        n = len(tl)
        S.op("dve", lambda: V.tensor_scalar(out=stat[:, r, 1, 0:n], in0=stat[:, r, 0, 0:n], scalar1=1.0 / D,
                                            scalar2=EPS, op0=ALU.mult, op1=ALU.add),
             reads=[sk], writes=[sk])
        S.op("act", lambda: A.activation(out=stat[:, r, 2, 0:n], in_=stat[:, r, 1, 0:n], func=AF.Sqrt),
             reads=[sk], writes=[sk])
        S.op("dve", lambda: V.reciprocal(out=stat[:, r, 3, 0:n], in_=stat[:, r, 2, 0:n]),
             reads=[sk], writes=[sk])
        return r

    def rms_apply(t, r, i):
        S.op("act", lambda: A.activation(out=xn[:], in_=xt[:, t, :], func=AF.Copy,
                                         scale=stat[:, r, 3, i:i + 1]),
             reads=[("x", t), ("stat", r)], writes=XNALL)

    def norm_to_hT(g, stage, vrow):
        tl = tiles_of(g, stage)
        r = rms_batch(tl)
        for i, t in enumerate(tl):
            rms_apply(t, r, i)
            for q in range(4):
                b = bank()

                def f():
                    for c in range(4):
                        ins = PE.transpose(ps[:, b, c * 128:(c + 1) * 128],
                                           xn[:, (4 * q + c) * 128:(4 * q + c + 1) * 128], ident[:])
                    return ins
                S.op("pe", f, reads=XNALL + ["ident"], writes=[("ps", b)])
                gb = vfm[:, vrow * 16 + 4 * q: vrow * 16 + 4 * q + 4].unsqueeze(2).broadcast_to([128, 4, 128])
                S.op("dve", lambda: V.tensor_tensor(
                    out=hT[:, 4 * q:4 * q + 4, t * 128:(t + 1) * 128],
                    in0=ps[:, b, :].rearrange("p (c n) -> p c n", c=4), in1=gb, op=ALU.mult),
                    reads=[("ps", b), "vfm"], writes=[("hT", t)])

    def out_proj(g, stage, wname, widx, r0, bias_in_T4=False):
        for cb in range(4):
            def consume(slot, cb=cb):
                for t in tiles_of(g, stage):
                    b = tm_unit(slot, Y, ("Y", t), t)
                    S.op("dve", lambda: V.tensor_tensor(out=xt[:, t, cb * 512:(cb + 1) * 512],
                                                        in0=ps[:, b, :], in1=xt[:, t, cb * 512:(cb + 1) * 512],
                                                        op=ALU.add),
                         reads=[("ps", b), ("x", t)], writes=[("x", t)])
                    if bias_in_T4:
                        S.op("dve", lambda: V.tensor_tensor(out=xt[:, t, cb * 512:(cb + 1) * 512],
                                                            in0=xt[:, t, cb * 512:(cb + 1) * 512],
                                                            in1=T4[:, cb, 0:512], op=ALU.add),
                             reads=[("x", t)] + T4ALL, writes=[("x", t)])
            add_w(wsrc(wname, widx, r0, cb * 512), consume)

    def ffn(g, stage, l):
        def prep_f():
            S.alias_barrier(YKEYS)
            norm_to_hT(g, stage, V_NFFN + l)
        add(prep_f)
        rot = [0]
        for j in range(4):
            for cb in range(4):
                def consume(slot, cb=cb):
                    for mi in range(4):
                        for (t0, n) in tblocks(g, stage):
                            b = fm_unit(slot, mi, hT, hkeys(t0, n), t0, n)
                            r = rot[0] % 4
                            rot[0] += 1
                            S.op("act", lambda: A.activation(out=T4[:, r, 0:n], in_=ps[:, b, 0:n], func=AF.Relu),
                                 reads=[("ps", b)], writes=[("T4", r)])
                            S.op("dve", lambda: V.tensor_tensor(out=Y[:, cb * 4 + mi, t0:t0 + n],
                                                                in0=T4[:, r, 0:n], in1=T4[:, r, 0:n], op=ALU.mult),
                                 reads=[("T4", r)], writes=ykeys(t0, n))
                add_w(wsrc("ffn_w_up", l, 0, j * 2048 + cb * 512), consume)
            out_proj(g, stage, "ffn_w_down", l, j * 2048)

    def mixer_a(g, stage, l):
        j = l // 3
        v32 = R[:, 0:NT * D].rearrange("p (t f) -> p t f", t=NT)
        tl = tiles_of(g, stage)
        sample_tile = 4 if g == 1 else None
        nkind = 2 if g == 1 else 1

        def prep_a():
            S.alias_barrier(RKEYS)
            norm_to_hT(g, stage, V_NMIX + l)
        add(prep_a)

        for cb in range(4):
            def consume(slot, cb=cb):
                for t in tl:
                    b = tm_unit(slot, hT, ("hT", t), t)
                    S.op("act", lambda: A.activation(out=v32[:, t, cb * 512:(cb + 1) * 512], in_=ps[:, b, :], func=AF.Gelu),
                         reads=[("ps", b)], writes=[("vq", t, cb)])
                    S.op("dve", lambda: V.bn_stats(out=bnst[:, t, cb, :], in_=v32[:, t, cb * 512:(cb + 1) * 512]),
                         reads=[("vq", t, cb)], writes=[("bn", t)])
            add_w(wsrc("a_w_in", j, 0, D + cb * 512), consume)

        def ln_quarter_load(q):
            cs = slice(q * 512, (q + 1) * 512)
            i_ = 0
            S.dma("sp", lnp[:, i_, 0, :], W["a_ln_g"][j:j + 1, cs].broadcast_to([128, 512]), writes=[("lnp", i_)])
            S.dma("sp", lnp[:, i_, 1, :], W["a_ln_b"][j:j + 1, cs].broadcast_to([128, 512]), writes=[("lnp", i_)])

        def ln_quarter_tile(q, t):
            cs = slice(q * 512, (q + 1) * 512)
            i_ = 0
            S.op("act", lambda: A.activation(out=v32[:, t, cs], in_=v32[:, t, cs], func=AF.Identity,
                                             bias=mv[:, t, 2:3], scale=mv[:, t, 1:2]),
                 reads=[("vq", t, q), "mv"], writes=[("vq", t, q)])
            S.op("dve", lambda: V.tensor_tensor(out=v32[:, t, cs], in0=v32[:, t, cs], in1=lnp[:, i_, 0, :], op=ALU.mult),
                 reads=[("vq", t, q), ("lnp", i_)], writes=[("vq", t, q)])
            S.op("dve", lambda: V.tensor_tensor(out=v32[:, t, cs], in0=v32[:, t, cs], in1=lnp[:, i_, 1, :], op=ALU.add),
                 reads=[("vq", t, q), ("lnp", i_)], writes=[("vq", t, q)])
            if q == 3 and t == tl[-1] and sample_tile is not None:
                S.dma("sp", av_out[j], v32[:, sample_tile, :],
                      reads=[("vq", sample_tile, q_) for q_ in range(4)], is_output=True)

        def ln_quarter(q):
            ln_quarter_load(q)
            for t in tl:
                ln_quarter_tile(q, t)

        def ln_v():
            prep_group(0)
            for t in tl:
                S.op("dve", lambda: V.bn_aggr(out=mv[:, t, 0:2], in_=bnst[:, t].rearrange("p a b -> p (a b)")),
                     reads=[("bn", t)], writes=["mv"])
            t0_, t1_ = tl[0], tl[-1] + 1
            S.op("act", lambda: A.activation(out=mv[:, t0_:t1_, 1:2], in_=mv[:, t0_:t1_, 1:2], func=AF.Sqrt, bias=EPS),
                 reads=["mv"], writes=["mv"])
            S.op("dve", lambda: V.reciprocal(out=mv[:, t0_:t1_, 1:2], in_=mv[:, t0_:t1_, 1:2]),
                 reads=["mv"], writes=["mv"])
            S.op("dve", lambda: V.scalar_tensor_tensor(out=mv[:, t0_:t1_, 2:3], in0=mv[:, t0_:t1_, 0:1], scalar=-1.0,
                                                       in1=mv[:, t0_:t1_, 1:2], op0=ALU.mult, op1=ALU.mult),
                 reads=["mv"], writes=["mv"])
            ln_quarter(0)
        add(ln_v)

        def prep_group(grp):
            i = grp % 2
            for kind in range(nkind):
                if kind == 0:
                    S.dma("sp", wld[:, kind, :], W["a_w_s"][j, grp], writes=[("wld", kind)])
                    S.op("dve", lambda: V.tensor_tensor(out=wld[:, kind, :], in0=wld[:, kind, :], in1=maskP[:], op=ALU.mult),
                         reads=[("wld", kind), "maskP"], writes=[("wld", kind)])
                    S.dma("sp", bsb[:, i, 0, :], W["a_b_s"][j, grp:grp + 1, :].broadcast_to([128, 128]), writes=[("bsr", i)])
                else:
                    S.dma("sp", wrep[:, 0, :], W["a_ws8"][j, grp], writes=["wrep"])
                    S.op("dve", lambda: V.tensor_tensor(
                        out=wld[:, kind, :].rearrange("p (a b) -> p a b", a=16),
                        in0=wrep[:, 0, :].unsqueeze(1).broadcast_to([128, 16, 8]),
                        in1=maskS[:].rearrange("p (a b) -> p a b", a=16), op=ALU.mult),
                        reads=["wrep", "maskS"], writes=[("wld", kind)])
                    S.dma("sp", bsb[:, i, 1, :], W["a_bs8"][j, grp:grp + 1, :].broadcast_to([128, 128]), writes=[("bsr", i)])
                b = bank()
                S.op("pe", lambda: PE.transpose(ps[:, b, 0:128], wld[:, kind, :], ident[:]),
                     reads=[("wld", kind), "ident"], writes=[("ps", b)])
                S.op("act", lambda: A.activation(out=wsTb[:, i, kind, :], in_=ps[:, b, 0:128], func=AF.Copy),
                     reads=[("ps", b)], writes=[("wsT", i)])

        for cb in range(4):
            def consume(slot, cb=cb):
                for mi in range(4):
                    m = cb * 4 + mi
                    grp = m // 2
                    i = grp % 2
                    if mi == 0:
                        lnq = list(tl) if cb + 1 < 4 else []
                        if lnq:
                            ln_quarter_load(cb + 1)
                    if m % 2 == 0 and grp + 1 < 8:
                        prep_group(grp + 1)
                    for (t0, n) in tblocks(g, stage):
                        bu = fm_unit(slot, mi, hT, hkeys(t0, n), t0, n)
                        bg = bank()
                        trange = list(range(t0 // 128, (t0 + n + 127) // 128))

                        def gate():
                            for t in trange:
                                kd = 1 if t == sample_tile else 0
                                lo, hi = max(t * 128, t0), min((t + 1) * 128, t0 + n)
                                o = ps[:, bg, lo - t0:hi - t0]
                                ins = PE.matmul(o, v32[:, t, m * 128:(m + 1) * 128], wsTb[:, i, kd, lo - t * 128:hi - t * 128],
                                                start=True, stop=True)
                            return ins
                        S.op("pe", gate, reads=[("vq", t, cb) for t in trange] + [("wsT", i)],
                             writes=[("ps", bg)])
                        segs_ = []
                        for t in trange:
                            kd = 1 if t == sample_tile else 0
                            lo, hi = max(t * 128, t0), min((t + 1) * 128, t0 + n)
                            full = (kd == 0 and hi - lo == 128)
                            if full and segs_ and segs_[-1][0] == "full":
                                segs_[-1][2] += 1
                            else:
                                segs_.append(["full" if full else "part", lo, 1, hi, kd, t])
                        for kind_, lo, cnt_, hi, kd, t in segs_:
                            if kind_ == "full":
                                o_ = ps[:, bg, lo - t0:lo - t0 + cnt_ * 128].rearrange("p (a c) -> p a c", c=128)
                                i1_ = bsb[:, i, 0, :].unsqueeze(1).broadcast_to([128, cnt_, 128])
                            else:
                                o_ = ps[:, bg, lo - t0:hi - t0]
                                i1_ = bsb[:, i, kd, lo - t * 128:hi - t * 128]
                            S.op("dve", lambda: V.tensor_tensor(out=o_, in0=o_, in1=i1_, op=ALU.add),
                                 reads=[("ps", bg), ("bsr", i)], writes=[("ps", bg)])
                        r = (m * 2 + (0 if t0 < 384 else 1)) % 4
                        S.op("act", lambda: A.activation(out=T4[:, r, 0:n], in_=ps[:, bu, 0:n], func=AF.Gelu),
                             reads=[("ps", bu)], writes=[("T4", r)])
                        S.op("dve", lambda: V.tensor_tensor(out=Y[:, m, t0:t0 + n], in0=ps[:, bg, 0:n],
                                                            in1=T4[:, r, 0:n], op=ALU.mult),
                             reads=[("ps", bg), ("T4", r)], writes=ykeys(t0, n))
                        if cb + 1 < 4 and lnq:
                            ln_quarter_tile(cb + 1, lnq.pop(0))
                while cb + 1 < 4 and lnq:
                    ln_quarter_tile(cb + 1, lnq.pop(0))
            add_w(wsrc("a_w_in", j, 0, cb * 512), consume)
        out_proj(g, stage, "a_w_out", j, 0)

    def mixer_b(g, l):
        stage = 0
        NP = 640 if g == 0 else 512
        ci_p = R[:, 0:16 * (NP + 2)].rearrange("p (c w) -> p c w", c=16)
        ci_s = None
        if g == 1:
            o0 = 16 * (NP + 2)
            ci_s = R[:, o0:o0 + 16 * 160].rearrange("p (c s w) -> p c s w", c=16, s=16)

        def segs(t0, n):
            if g == 0 or t0 + n <= 512:
                return [(0, n, False)]
            return [(0, 512 - t0, False), (512 - t0, 128, True)]

        def prep():
            S.alias_barrier(RKEYS)
            norm_to_hT(g, stage, V_NMIX + l)
            S.op("dve", lambda: V.tensor_copy(out=ci_p[:, :, 0:2], in_=bpast[:]), reads=["bpast"], writes=["cipast"])
            if g == 1:
                S.dma("sp", xn[0:32, :], sb_in, writes=XNALL)
                for q in range(4):
                    b = bank()

                    def f():
                        for c in range(4):
                            ins = PE.transpose(ps[:, b, c * 32:(c + 1) * 32],
                                               xn[0:32, (4 * q + c) * 128:(4 * q + c + 1) * 128], ident[0:32, 0:32])
                        return ins
                    S.op("pe", f, reads=XNALL + ["ident"], writes=[("ps", b)])
                    S.op("dve", lambda: V.tensor_copy(
                        out=ci_s[:, 4 * q:4 * q + 4, :, 0:2],
                        in_=ps[:, b, 0:128].rearrange("p (c s w) -> p c s w", c=4, s=16)),
                        reads=[("ps", b)], writes=["cipast"])
        add(prep)

        for cb in range(4):
            def c_hx(slot, cb=cb):
                for mi in range(4):
                    for (t0, n) in tblocks(g, stage):
                        b = fm_unit(slot, mi, hT, hkeys(t0, n), t0, n)
                        S.op("act", lambda: A.activation(out=T4[:, mi, t0:t0 + n], in_=ps[:, b, 0:n], func=AF.Copy),
                             reads=[("ps", b)], writes=[("T4", mi)])
            add_w(wsrc("b_w_in", 0, 0, 2 * D + cb * 512), c_hx)

            def c_c(slot, cb=cb):
                for mi in range(4):
                    m = cb * 4 + mi
                    for (t0, n) in tblocks(g, stage):
                        b = fm_unit(slot, mi, hT, hkeys(t0, n), t0, n)
                        for (o, ln, smp) in segs(t0, n):
                            if not smp:
                                S.op("dve", lambda: V.tensor_tensor(out=ci_p[:, m, 2 + t0 + o:2 + t0 + o + ln],
                                                                    in0=ps[:, b, o:o + ln], in1=T4[:, mi, t0 + o:t0 + o + ln],
                                                                    op=ALU.mult),
                                     reads=[("ps", b), ("T4", mi)], writes=[("ci", m)])
                            else:
                                S.op("dve", lambda: V.tensor_tensor(
                                    out=ci_s[:, m, :, 2:10],
                                    in0=ps[:, b, o:o + ln].rearrange("p (s w) -> p s w", s=16),
                                    in1=T4[:, mi, t0 + o:t0 + o + ln].rearrange("p (s w) -> p s w", s=16), op=ALU.mult),
                                    reads=[("ps", b), ("T4", mi)], writes=[("ci", m)])
                for k in range(3):
                    for smp in ((False, True) if g == 1 else (False,)):
                        for mi in range(4):
                            m = cb * 4 + mi
                            rk = [("ci", m), "cipast", "vfm"]
                            if smp:
                                acc = T4[:, mi, 512:640].rearrange("p (s w) -> p s w", s=16)
                                src = ci_s[:, m, :, k:k + 8]
                            else:
                                acc = T4[:, mi, 0:NP]
                                src = ci_p[:, m, k:k + NP]
                            if k == 0:
                                S.op("dve", lambda: V.tensor_scalar(out=acc, in0=src, scalar1=vcol(V_BCW, m),
                                                                    scalar2=None, op0=ALU.mult),
                                     reads=rk + [("T4", mi)], writes=[("T4", mi)])
                            else:
                                S.op("dve", lambda: V.scalar_tensor_tensor(out=acc, in0=src, scalar=vcol(V_BCW + k, m),
                                                                           in1=acc, op0=ALU.mult, op1=ALU.add),
                                     reads=rk + [("T4", mi)], writes=[("T4", mi)])
            add_w(wsrc("b_w_in", 0, 0, D + cb * 512), c_c)

            def c_b(slot, cb=cb):
                for mi in range(4):
                    m = cb * 4 + mi
                    for (t0, n) in tblocks(g, stage):
                        b = fm_unit(slot, mi, hT, hkeys(t0, n), t0, n)
                        S.op("dve", lambda: V.tensor_tensor(out=Y[:, m, t0:t0 + n], in0=ps[:, b, 0:n],
                                                            in1=T4[:, mi, t0:t0 + n], op=ALU.mult),
                             reads=[("ps", b), ("T4", mi)], writes=ykeys(t0, n))
            add_w(wsrc("b_w_in", 0, 0, cb * 512), c_b)

        def tails():
            allci = [("ci", m) for m in range(16)]
            S.op("dve", lambda: V.tensor_copy(out=bpast[:], in_=ci_p[:, :, NP:NP + 2]), reads=allci, writes=["bpast"])
            if g == 1:
                for q in range(4):
                    b = bank()

                    def f():
                        for c in range(4):
                            ins = PE.transpose(ps[0:2, b, c * 128:(c + 1) * 128], bpast[:, 4 * q + c, :], ident[:])
                        return ins
                    S.op("pe", f, reads=["bpast", "ident"], writes=[("ps", b)])
                    S.op("act", lambda: A.activation(out=xn[0:2, q * 512:(q + 1) * 512], in_=ps[0:2, b, :], func=AF.Copy),
                         reads=[("ps", b)], writes=XNALL)
                S.dma("sp", bp_out, xn[0:2, :], reads=XNALL, is_output=True)
                tl32 = T4[:, 0, 0:512].rearrange("p (c s w) -> p c s w", c=16, s=16)
                S.op("dve", lambda: V.tensor_copy(out=tl32, in_=ci_s[:, :, :, 8:10]), reads=allci, writes=[("T4", 0)])
                for q in range(4):
                    b = bank()

                    def f2():
                        for c in range(4):
                            ins = PE.transpose(ps[0:32, b, c * 128:(c + 1) * 128],
                                               T4[:, 0, (4 * q + c) * 32:(4 * q + c + 1) * 32], ident[:])
                        return ins
                    S.op("pe", f2, reads=[("T4", 0), "ident"], writes=[("ps", b)])
                    S.op("act", lambda: A.activation(out=xn[0:32, q * 512:(q + 1) * 512], in_=ps[0:32, b, :], func=AF.Copy),
                         reads=[("ps", b)], writes=XNALL)
                S.dma("sp", bs_out, xn[0:32, :], reads=XNALL, is_output=True)
        add(tails)
        out_proj(g, stage, "b_w_out", 0, 0)

    def mixer_c(g, l):
        GL = R[:, 0:16 * 670].rearrange("p (c w) -> p c w", c=16)
        NP = 640 if g == 0 else 512
        C0 = 128 if g == 0 else 0
        stl = xn[0:120, 0:512].rearrange("p (q f) -> p q f", q=4)
        Yf = Y[:].rearrange("p a b -> p (a b)").bitcast(F32)
        sgs = [xn[:, 512:1120].rearrange("p (s w) -> p s w", s=16),
               xn[:, 1120:1728].rearrange("p (s w) -> p s w", s=16),
               Yf[:, 3840:4448].rearrange("p (s w) -> p s w", s=16),
               Yf[:, 4448:5056].rearrange("p (s w) -> p s w", s=16)]
        sgkeys = [("xn", 1), ("xn", 2), ("sgy", 0), ("sgy", 1)]
        sgt = Yf[:, 0:480]
        cst = Yf[0:120, 512:1024].rearrange("p (q f) -> p q f", q=4)
        NDG = 22
        dg = Yf[:, 1024:1024 + NDG * 128].rearrange("p (s f) -> p s f", s=NDG)
        dg_ctr = [0]
        KPE = 8 if g == 0 else 11
        sbanks = []

        def segs(t0, n):
            if g == 0 or t0 + n <= 512:
                return [(0, n, False)]
            return [(0, 512 - t0, False), (512 - t0, 128, True)]

        def prep():
            S.alias_barrier(RKEYS)
            S.alias_barrier(YKEYS)
            norm_to_hT(g, 0, V_NMIX + l)
            S.op("dve", lambda: V.tensor_copy(out=GL[:, :, 0:30], in_=cpast[:]), reads=["cpast"], writes=["glpast"])
            for i in range(4):
                b = bank()
                bank_state["reserved"].add(b)
                sbanks.append(b)
        add(prep)

        wg_slot = {}
        deferred = []
        for cb in range(4):
            add_w(wsrc("c_w_pw1", 0, 0, D + cb * 512), lambda slot, cb=cb: wg_slot.__setitem__(cb, slot))

            def c_a(slot, cb=cb):
                n_out = 512
                gslot = wg_slot[cb]
                for p in range(2):
                    base = 2 * ((cb * 2 + p) % 2)
                    pair = (2 * p, 2 * p + 1)

                    def sl(mi):
                        return base + (mi % 2)
                    for mi in pair:
                        m = cb * 4 + mi
                        for (t0, n) in tblocks(g, 0):
                            b = fm_unit(gslot, mi, hT, hkeys(t0, n), t0, n)
                            S.op("act", lambda: A.activation(out=T4[:, sl(mi), t0:t0 + n], in_=ps[:, b, 0:n], func=AF.Sigmoid,
                                                             bias=vcol(V_CB1 + 1, m)),
                                 reads=[("ps", b), "vfm"], writes=[("T4", sl(mi))])
                    for mi in pair:
                        m = cb * 4 + mi
                        ts = sl(mi)
                        sg, sgk = sgs[ts], sgkeys[ts]
                        for (t0, n) in tblocks(g, 0):
                            b = fm_unit(slot, mi, hT, hkeys(t0, n), t0, n)
                            for (o, ln, smp) in segs(t0, n):
                                if not smp:
                                    S.op("dve", lambda: V.scalar_tensor_tensor(
                                        out=GL[:, m, 30 + t0 + o:30 + t0 + o + ln], in0=ps[:, b, o:o + ln],
                                        scalar=vcol(V_CB1, m), in1=T4[:, ts, t0 + o:t0 + o + ln],
                                        op0=ALU.add, op1=ALU.mult),
                                        reads=[("ps", b), ("T4", ts), "vfm"], writes=[("gl", m)])
                                else:
                                    S.op("dve", lambda: V.scalar_tensor_tensor(
                                        out=sg[:, :, 30:38], in0=ps[:, b, o:o + ln].rearrange("p (s w) -> p s w", s=16),
                                        scalar=vcol(V_CB1, m), in1=T4[:, ts, t0 + o:t0 + o + ln].rearrange("p (s w) -> p s w", s=16),
                                        op0=ALU.add, op1=ALU.mult),
                                        reads=[("ps", b), ("T4", ts), "vfm"], writes=[sgk])
                        if g == 0:
                            S.op("dve", lambda: V.tensor_scalar(out=GL[:, m, 30:158], in0=GL[:, m, 30:158],
                                                                scalar1=hmask[:, 0:1], scalar2=None, op0=ALU.mult),
                                 reads=[("gl", m), "hmask"], writes=[("gl", m)])
                        else:
                            S.dma("sp", stl, sc_in.rearrange("(q r) f -> r q f", q=4)[:, :, m * 128:(m + 1) * 128],
                                  writes=[("xn", 0)])
                            b = bank()

                            def f():
                                for q in range(4):
                                    ins = PE.transpose(ps[:, b, q * 120:(q + 1) * 120], stl[:, q, :], ident[0:120, 0:120])
                                return ins
                            S.op("pe", f, reads=[("xn", 0), "ident"], writes=[("ps", b)])
                            S.op("act", lambda: A.activation(out=sg[:, :, 0:30],
                                                             in_=ps[:, b, 0:480].rearrange("p (s r) -> p s r", s=16), func=AF.Copy),
                                 reads=[("ps", b)], writes=[sgk])
                    cbanks = {}
                    for mi in pair:
                        m = cb * 4 + mi
                        bc = bank()
                        cbanks[mi] = bc
                        taps = list(range(31 - KPE, 31))
                        sls = [(dg_ctr[0] + i_) % NDG for i_ in range(KPE)]
                        dg_ctr[0] += KPE

                        def build():
                            for k, sl_ in zip(taps, sls):
                                ins = A.activation(out=dg[:, sl_, :], in_=ident[:], func=AF.Copy, scale=vcol(V_CCW + k, m))
                            return ins
                        S.op("act", build, reads=["ident", "vfm"], writes=[("dg", sl_) for sl_ in sls])

                        def ptaps():
                            for k, sl_ in zip(taps, sls):
                                ins = PE.matmul(ps[:, bc, :], dg[:, sl_, :], GL[:, m, C0 + k:C0 + k + n_out],
                                                start=(k == 31 - KPE), stop=(k == 30))
                            return ins
                        S.op("pe", ptaps, reads=[("dg", sl_) for sl_ in sls] + [("gl", m), "glpast"], writes=[("ps", bc)])
                    for fn_ in deferred:
                        fn_()
                    deferred.clear()
                    kadd = min(12, 30 - KPE)
                    for k in range(31):
                        for smp in ((False, True) if g == 1 else (False,)):
                            if (not smp) and k >= 31 - KPE:
                                continue
                            for mi in pair:
                                m = cb * 4 + mi
                                ts = sl(mi)
                                if smp:
                                    acc = T4[:, ts, 512:640].rearrange("p (s w) -> p s w", s=16)
                                    src = sgs[ts][:, :, k:k + 8]
                                    rk = [sgkeys[ts], "vfm", ("T4", ts)]
                                else:
                                    acc = T4[:, ts, C0:C0 + n_out]
                                    src = GL[:, m, C0 + k:C0 + k + n_out]
                                    rk = [("gl", m), "glpast", "vfm", ("T4", ts)]
                                if k == 0:
                                    S.op("dve", lambda: V.tensor_scalar(out=acc, in0=src, scalar1=vcol(V_CCW, m),
                                                                        scalar2=None, op0=ALU.mult),
                                         reads=rk, writes=[("T4", ts)])
                                else:
                                    S.op("dve", lambda: V.scalar_tensor_tensor(out=acc, in0=src, scalar=vcol(V_CCW + k, m),
                                                                               in1=acc, op0=ALU.mult, op1=ALU.add),
                                         reads=rk, writes=[("T4", ts)])
                        if k == kadd:
                            for mi in pair:
                                ts = sl(mi)
                                acc = T4[:, ts, C0:C0 + n_out]
                                S.op("dve", lambda: V.tensor_tensor(out=acc, in0=ps[:, cbanks[mi], :], in1=acc, op=ALU.add),
                                     reads=[("ps", cbanks[mi]), ("T4", ts)], writes=[("T4", ts)])
                    for mi in pair:
                        m = cb * 4 + mi
                        ts = sl(mi)
                        sg, sgk = sgs[ts], sgkeys[ts]
                        if g == 1:
                            def tail(m=m, sg=sg, sgk=sgk):
                                b2 = bank()
                                S.op("act", lambda: A.activation(out=sgt[:, 0:480].rearrange("p (s r) -> p s r", s=16),
                                                                 in_=sg[:, :, 8:38], func=AF.Copy),
                                     reads=[sgk], writes=["sgt"])

                                def f3():
                                    for q in range(4):
                                        ins = PE.transpose(ps[0:120, b2, q * 128:(q + 1) * 128], sgt[:, q * 120:(q + 1) * 120], ident[:])
                                    return ins
                                S.op("pe", f3, reads=["sgt", "ident"], writes=[("ps", b2)])
                                S.op("act", lambda: A.activation(out=cst, in_=ps[0:120, b2, :].rearrange("p (q f) -> p q f", q=4),
                                                                 func=AF.Copy),
                                     reads=[("ps", b2)], writes=["cst"])
                                S.dma("sp", cs_out.rearrange("(q r) f -> r q f", q=4)[:, :, m * 128:(m + 1) * 128], cst,
                                      reads=["cst"], is_output=True)
                            deferred.append(tail)
                        S.op("dve", lambda: V.tensor_copy(out=cpast[:, m, :], in_=GL[:, m, NP:NP + 30]),
                             reads=[("gl", m), "glpast"], writes=["cpast"])
                        S.op("dve", lambda: V.tensor_scalar(out=GL[:, m, C0:640], in0=T4[:, ts, C0:640],
                                                            scalar1=vcol(V_CCB, m), scalar2=None, op0=ALU.add),
                             reads=[("T4", ts), "vfm", "cpast"], writes=[("gl", m)])
                        S.op("act", lambda: A.activation(out=T4[:, ts, C0:640], in_=GL[:, m, C0:640], func=AF.Square),
                             reads=[("gl", m)], writes=[("T4", ts)])
                        def stats(m=m, ts=ts):
                            for ib, (t0, n) in enumerate(tblocks(g, 1)):
                                def f4():
                                    PE.matmul(ps[:, sbanks[ib], 0:n], onesF[:], GL[:, m, t0:t0 + n], start=(m == 0), stop=(m == 15))
                                    return PE.matmul(ps[:, sbanks[2 + ib], 0:n], onesF[:], T4[:, ts, t0:t0 + n],
                                                     start=(m == 0), stop=(m == 15))
                                S.op("pe", f4, reads=[("gl", m), ("T4", ts), "onesF"],
                                     writes=[("ps", sbanks[ib]), ("ps", sbanks[2 + ib])])
                        deferred.append(stats)
            add_w(wsrc("c_w_pw1", 0, 0, cb * 512), c_a, hold=1)

        def ln_c():
            for fn_ in deferred:
                fn_()
            deferred.clear()
            for ib, (t0, n) in enumerate(tblocks(g, 1)):
                sl = slice(t0, t0 + n)
                b1, b2 = sbanks[ib], sbanks[2 + ib]
                S.op("act", lambda: A.activation(out=T4[:, 0, sl], in_=ps[:, b1, 0:n], func=AF.Copy, scale=1.0 / D),
                     reads=[("ps", b1)], writes=[("T4", 0)])
                S.op("dve", lambda: V.tensor_tensor(out=T4[:, 1, sl], in0=T4[:, 0, sl], in1=T4[:, 0, sl], op=ALU.mult),
                     reads=[("T4", 0)], writes=[("T4", 1)])
                S.op("dve", lambda: V.scalar_tensor_tensor(out=T4[:, 2, sl], in0=ps[:, b2, 0:n], scalar=1.0 / D,
                                                           in1=T4[:, 1, sl], op0=ALU.mult, op1=ALU.subtract),
                     reads=[("ps", b2), ("T4", 1)], writes=[("T4", 2)])
                S.op("act", lambda: A.activation(out=T4[:, 2, sl], in_=T4[:, 2, sl], func=AF.Sqrt, bias=EPS),
                     reads=[("T4", 2)], writes=[("T4", 2)])
                S.op("dve", lambda: V.reciprocal(out=T4[:, 2, sl], in_=T4[:, 2, sl]),
                     reads=[("T4", 2)], writes=[("T4", 2)])
            for b in sbanks:
                bank_state["reserved"].discard(b)
            for m in range(16):
                S.op("dve", lambda: V.tensor_tensor(out=GL[:, m, C0:640], in0=GL[:, m, C0:640], in1=T4[:, 0, C0:640],
                                                    op=ALU.subtract),
                     reads=[("gl", m), ("T4", 0)], writes=[("gl", m)])
                S.op("dve", lambda: V.tensor_tensor(out=GL[:, m, C0:640], in0=GL[:, m, C0:640], in1=T4[:, 2, C0:640],
                                                    op=ALU.mult),
                     reads=[("gl", m), ("T4", 2)], writes=[("gl", m)])
                S.op("act", lambda: A.activation(out=Y[:, m, C0:640], in_=GL[:, m, C0:640], func=AF.Silu,
                                                 bias=vcol(V_CLB, m), scale=vcol(V_CLG, m)),
                     reads=[("gl", m), "vfm"], writes=ykeys(C0, 640 - C0) + ["sgt", "cst", ("sgy", 0), ("sgy", 1)] + [("dg", i_) for i_ in range(22)])
            if g == 1:
                for q in range(4):
                    b = bank()

                    def f():
                        for c in range(4):
                            ins = PE.transpose(ps[0:30, b, c * 128:(c + 1) * 128], cpast[:, 4 * q + c, :], ident[:])
                        return ins
                    S.op("pe", f, reads=["cpast", "ident"], writes=[("ps", b)])
                    S.op("act", lambda: A.activation(out=xn[0:30, q * 512:(q + 1) * 512], in_=ps[0:30, b, :], func=AF.Copy),
                         reads=[("ps", b)], writes=XNALL)
                S.dma("sp", cp_out, xn[0:30, :], reads=XNALL, is_output=True)
            for q in range(4):
                S.dma("sp", T4[:, q, 0:512], W["c_b_pw2"][0:1, q * 512:(q + 1) * 512].broadcast_to([128, 512]),
                      writes=[("T4", q)])
        add(ln_c)
        out_proj(g, 1, "c_w_pw2", 0, 0, bias_in_T4=True)

    def final_out(g):
        rst = R[:, 0:NT * D].rearrange("p (t f) -> p t f", t=NT)
        rkeys_all = ([("ci", m) for m in range(16)] + [("gl", m) for m in range(16)] + ["cipast", "glpast"])

        def f():
            S.alias_barrier(RKEYS)
            for q in range(4):
                S.dma("sp", T4[:, q, 0:512], fin_g[0:1, q * 512:(q + 1) * 512].broadcast_to([128, 512]), writes=T4ALL)
            tl = tiles_of(g, 1)
            r = rms_batch(tl)
            for i, t in enumerate(tl):
                rms_apply(t, r, i)
                rk = [("vq", t, q_) for q_ in range(4)]
                S.op("dve", lambda: V.tensor_tensor(out=rst[:, t, :].rearrange("p (a b) -> p a b", a=4),
                                                    in0=xn[:].rearrange("p (a b) -> p a b", a=4),
                                                    in1=T4[:, :, 0:512], op=ALU.mult),
                     reads=XNALL + T4ALL, writes=rk + (rkeys_all if i == 0 else []))
                r0 = (g * NT + t - 1) * 128
                S.dma("sp", y_out[r0:r0 + 128, :], rst[:, t, :], reads=rk, is_output=True)
        add(f)

    add(setup)
    for g in groups:
        add(lambda g=g: load_group(g))
        for l in layers:
            kind = l % 3
            stage = 0 if l < 2 else (1 if l > 2 else 0)
            if kind == 0:
                mixer_a(g, stage, l)
            elif kind == 1:
                mixer_b(g, l)
            else:
                mixer_c(g, l)
            ffn(g, 1 if l >= 2 else 0, l)
        if dbg:
            add(lambda g=g: dump_dbg(g))
        if final:
            final_out(g)
    run_steps()
    S.finish()
    es.close()
    return nc


def make_consts():
    t = np.arange(128)
    ident = np.eye(128, dtype=np.float32)
    tril = (t[None, :] <= t[:, None]).astype(np.float32)
    blk = ((t[:, None] // 8 == t[None, :] // 8) & (t[None, :] % 8 <= t[:, None] % 8)).astype(np.float32)
    return np.stack([ident, tril, blk]).astype(np.float32)


def make_core_inputs(inp, c):
    seq, half = c // 2, c % 2
    xp = inp["x_prompt"][seq]
    main = xp[half * 1024:(half + 1) * 1024]
    halo = xp[896:1024] if half == 1 else np.zeros((128, D), np.float32)
    smp = inp["x_sample"][c * 16:(c + 1) * 16].reshape(128, D)
    xin = np.concatenate([halo, main, smp], axis=0).astype(np.float32)
    vecs = np.concatenate([
        inp["norm_mix_g"], inp["norm_ffn_g"], inp["b_conv_w"][0], inp["c_b_pw1"][0].reshape(2, D),
        inp["c_conv_w"][0], inp["c_conv_b"], inp["c_ln_g"], inp["c_ln_b"],
        np.zeros((1, D), np.float32)], axis=0).astype(np.float32)
    assert vecs.shape[0] == NVEC
    m = dict(
        xin=np.ascontiguousarray(xin),
        sb_in=np.ascontiguousarray(inp["state_b_conv"][0, c * 16:(c + 1) * 16].reshape(32, D)),
        sc_in=np.ascontiguousarray(inp["state_c_conv"][0, c * 16:(c + 1) * 16].reshape(480, D)),
        hmask=np.full((128, 1), float(half), np.float32),
        vecs=np.ascontiguousarray(vecs.reshape(NVEC * 16, 128)),
        consts=make_consts(),
        final_norm_g=np.ascontiguousarray(inp["final_norm_g"].reshape(1, D)),
    )
    return m


def derived_inputs(inp):
    return dict(a_ws8=np.ascontiguousarray(np.tile(inp["a_w_s"][:, :, 0:8, 0:8], (1, 1, 16, 1))),
                a_bs8=np.ascontiguousarray(np.tile(inp["a_b_s"][:, :, 0:8], (1, 1, 16))))


WEIGHT_NAMES = ["a_ws8", "a_bs8", "a_w_in", "a_ln_g", "a_ln_b", "a_w_s", "a_b_s", "a_w_out", "b_w_in", "b_w_out",
                "c_w_pw1", "c_w_pw2", "c_b_pw2", "ffn_w_up", "ffn_w_down"]


def kernel(**inp):
    inp = {k: np.asarray(v) for k, v in inp.items()}
    nc = build()
    inp.update(derived_inputs(inp))
    in_maps = []
    for c in range(NCORES):
        m = make_core_inputs(inp, c)
        for k in WEIGHT_NAMES:
            m[k] = np.ascontiguousarray(inp[k], dtype=np.float32)
        in_maps.append(m)
    res = run_bass_kernel_spmd(nc, in_maps, core_ids=list(range(NCORES)))
    r = res.results
    y_prompt = np.zeros((4, 2048, D), np.float32)
    y_sample = np.zeros((128, 8, D), np.float32)
    new_a_v = np.zeros((2, 128, 8, D), np.float32)
    nb_p = np.zeros((1, 4, 2, D), np.float32)
    nb_s = np.zeros((1, 128, 2, D), np.float32)
    nc_p = np.zeros((1, 4, 30, D), np.float32)
    nc_s = np.zeros((1, 128, 30, D), np.float32)
    for c in range(NCORES):
        seq, half = c // 2, c % 2
        y = r[c]["y_out"]
        y_prompt[seq, half * 1024:(half + 1) * 1024] = y[0:1024]
        y_sample[c * 16:(c + 1) * 16] = y[1024:1152].reshape(16, 8, D)
        new_a_v[:, c * 16:(c + 1) * 16] = r[c]["av_out"].reshape(2, 16, 8, D)
        nb_s[0, c * 16:(c + 1) * 16] = r[c]["bs_out"].reshape(16, 2, D)
        nc_s[0, c * 16:(c + 1) * 16] = r[c]["cs_out"].reshape(16, 30, D)
        if half == 1:
            nb_p[0, seq] = r[c]["bp_out"]
            nc_p[0, seq] = r[c]["cp_out"]
    return (y_prompt, y_sample, new_a_v, nb_p, nb_s, nc_p, nc_s)
```

```python
import numpy as np
from contextlib import ExitStack
import concourse.bass as bass
import concourse.mybir as mybir
from concourse.bass_utils import run_bass_kernel_spmd

F32 = mybir.dt.float32
BF16 = mybir.dt.bfloat16
AF = mybir.ActivationFunctionType
ALU = mybir.AluOpType

D = 2048
KC = 16
NT = 5
TG = 640
EPS = 1e-6
NCORES = 8
NRING = 3

V_NMIX, V_NFFN, V_BCW, V_CB1, V_CCW, V_CCB, V_CLG, V_CLB = 0, 4, 8, 11, 13, 44, 45, 46
NVEC = 48


class Sched:
    def __init__(self, nc):
        self.nc = nc
        self.sems = []
        self.q = {}
        for name, h in (("pe", nc.tensor), ("act", nc.scalar), ("dve", nc.vector),
                        ("pool", nc.gpsimd), ("sp", nc.sync)):
            self.q[name] = dict(h=h, waited={}, sem=self._sem("q_" + name), cnt=0)
        self.dq = {}
        for name in ("pool", "sp"):
            self.dq[name] = dict(sems=[self._sem(f"d_{name}{i}") for i in range(8)], i=0)
        self.lastw = {}
        self.readers = {}
        self.extra = {}
        self.out_tokens = []

    def _sem(self, name):
        s = self.nc.alloc_semaphore(name)
        self.sems.append(s)
        return len(self.sems) - 1

    def _wait(self, qn, toks):
        q = self.q[qn]
        for si, val in toks:
            if q["waited"].get(si, 0) < val:
                q["h"].wait_ge(self.sems[si], val)
                q["waited"][si] = val

    def _deps(self, qn, reads, writes):
        toks = {}

        def add(t):
            if t is not None and toks.get(t[0], 0) < t[1]:
                toks[t[0]] = t[1]
        for r in reads:
            add(self.lastw.get(r))
            for t in self.extra.get(r, {}).items():
                add(t)
        for w in writes:
            add(self.lastw.get(w))
            for t in self.readers.get(w, {}).items():
                add(t)
            for t in self.extra.get(w, {}).items():
                add(t)
        if qn == "pe":
            toks.pop(self.q["pe"]["sem"], None)
        return list(toks.items())

    def _commit(self, tok, reads, writes):
        for r in reads:
            d = self.readers.setdefault(r, {})
            if d.get(tok[0], 0) < tok[1]:
                d[tok[0]] = tok[1]
        for w in writes:
            self.lastw[w] = tok
            self.readers[w] = {}

    def alias_barrier(self, keys):
        toks = {}
        for k in keys:
            t = self.lastw.get(k)
            if t is not None and toks.get(t[0], 0) < t[1]:
                toks[t[0]] = t[1]
            for si, v in self.readers.get(k, {}).items():
                if toks.get(si, 0) < v:
                    toks[si] = v
            for si, v in self.extra.get(k, {}).items():
                if toks.get(si, 0) < v:
                    toks[si] = v
        for k in keys:
            self.extra[k] = dict(toks)

    def op(self, qn, fn, reads=(), writes=()):
        self._wait(qn, self._deps(qn, reads, writes))
        ins = fn()
        q = self.q[qn]
        q["cnt"] += 1
        ins.then_inc(self.sems[q["sem"]], 1)
        self._commit((q["sem"], q["cnt"]), reads, writes)

    def dma(self, qn, out, in_, reads=(), writes=(), is_output=False, **kw):
        dq = self.dq[qn]
        i = dq["i"]
        K = len(dq["sems"])
        si = dq["sems"][i % K]
        need = 16 * (i // K)
        toks = self._deps(qn, reads, writes)
        if need > 0:
            toks.append((si, need))
        self._wait(qn, toks)
        self.q[qn]["h"].dma_start(out=out, in_=in_, **kw).then_inc(self.sems[si], 16)
        dq["i"] += 1
        tok = (si, need + 16)
        self._commit(tok, reads, writes)
        if is_output:
            self.out_tokens.append(tok)

    def finish(self):
        toks = {}
        for si, v in self.out_tokens:
            toks[si] = max(toks.get(si, 0), v)
        for name, dq in self.dq.items():
            K = len(dq["sems"])
            for j in range(min(dq["i"], K)):
                idx = dq["i"] - 1 - j
                si = dq["sems"][idx % K]
                toks[si] = max(toks.get(si, 0), 16 * (idx // K + 1))
        for name in ("pe", "act", "dve"):
            q = self.q[name]
            if q["cnt"]:
                toks[q["sem"]] = q["cnt"]
        self._wait("sp", list(toks.items()))


def build(layers=(0, 1, 2, 3), groups=(0, 1), final=True, dbg=False):
    nc = bass.Bass("TRN2", target_bir_lowering=False)

    def din(name, shape):
        return nc.dram_tensor(name, list(shape), F32, kind="ExternalInput").ap()

    def dout(name, shape):
        return nc.dram_tensor(name, list(shape), F32, kind="ExternalOutput").ap()

    xin = din("xin", [1280, D])
    sb_in = din("sb_in", [32, D])
    sc_in = din("sc_in", [480, D])
    hmask_in = din("hmask", [128, 1])
    vecs_in = din("vecs", [NVEC * 16, 128])
    consts_in = din("consts", [3, 128, 128])
    fin_g = din("final_norm_g", [1, D])
    W = {}
    kinds = sorted(set(l % 3 for l in layers))
    if 0 in kinds:
        W["a_w_in"] = din("a_w_in", [2, D, 2 * D])
        W["a_ln_g"] = din("a_ln_g", [2, D])
        W["a_ln_b"] = din("a_ln_b", [2, D])
        W["a_w_s"] = din("a_w_s", [2, 8, 128, 128])
        W["a_b_s"] = din("a_b_s", [2, 8, 128])
        W["a_ws8"] = din("a_ws8", [2, 8, 128, 8])
        W["a_bs8"] = din("a_bs8", [2, 8, 128])
        W["a_w_out"] = din("a_w_out", [2, D, D])
    if 1 in kinds:
        W["b_w_in"] = din("b_w_in", [1, D, 3 * D])
        W["b_w_out"] = din("b_w_out", [1, D, D])
    if 2 in kinds:
        W["c_w_pw1"] = din("c_w_pw1", [1, D, 2 * D])
        W["c_w_pw2"] = din("c_w_pw2", [1, D, D])
        W["c_b_pw2"] = din("c_b_pw2", [1, D])
    W["ffn_w_up"] = din("ffn_w_up", [4, D, 4 * D])
    W["ffn_w_down"] = din("ffn_w_down", [4, 4 * D, D])

    y_out = dout("y_out", [1152, D])
    av_out = dout("av_out", [2, 128, D])
    bp_out = dout("bp_out", [2, D])
    bs_out = dout("bs_out", [32, D])
    cp_out = dout("cp_out", [30, D])
    cs_out = dout("cs_out", [480, D])
    xdbg = dout("xdbg", [1280, D]) if dbg else None

    es = ExitStack()

    def sb(name, shape, dt=F32):
        return es.enter_context(nc.sbuf_tensor(name, list(shape), dt))

    xt = sb("xt", [128, NT, D])
    hT = sb("hT", [128, KC, TG], BF16)
    Y = sb("Y", [128, KC, TG], BF16)
    R = sb("R", [128, 16 * 674])
    ring = sb("ring", [128, NRING, KC, 512], BF16)
    xn = sb("xn", [128, D])
    T4 = sb("T4", [128, 4, TG])
    ident = sb("ident", [128, 128])
    maskP = sb("maskP", [128, 128])
    maskS = sb("maskS", [128, 128])
    onesF = sb("onesF", [128, 128])
    wsTb = sb("wsTb", [128, 2, 2, 128])
    wld = sb("wld", [128, 2, 128])
    wrep = sb("wrep", [128, 2, 8])
    bsb = sb("bsb", [128, 2, 2, 128])
    vfm = sb("vfm", [128, NVEC * 16])
    stat = sb("stat", [128, 2, 4, 8])
    bnst = sb("bnst", [128, NT, 4, 6])
    mv = sb("mv", [128, NT, 4])
    hmask = sb("hmask_sb", [128, 1])
    lnp = sb("lnp", [128, 1, 2, 512])
    bpast = sb("bpast", [128, KC, 2])
    cpast = sb("cpast", [128, KC, 30])
    ps = es.enter_context(nc.psum_tensor("ps", [128, 8, 512], F32))

    S = Sched(nc)
    V, A, PE = nc.vector, nc.scalar, nc.tensor

    bank_state = dict(next=0, reserved=set())

    def bank():
        while True:
            b = bank_state["next"] % 8
            bank_state["next"] += 1
            if b not in bank_state["reserved"]:
                return b

    steps = []

    def wsrc(name, idx, r0, c0):
        w = W[name]
        return w[idx, r0:r0 + D, :].rearrange("(c p) n -> p c n", p=128)[:, :, c0:c0 + 512]

    def add_w(src, fn, hold=0):
        steps.append((src, fn, hold))

    def add(fn):
        steps.append((None, lambda slot: fn(), 0))

    def run_steps():
        wl = [i for i, s in enumerate(steps) if s[0] is not None]
        slot_of = {}
        issued = 0

        def issue_upto(n):
            nonlocal issued
            while issued < min(n, len(wl)):
                i = wl[issued]
                s = issued % NRING
                slot_of[i] = s
                S.dma("pool", ring[:, s], steps[i][0], writes=[("w", s)])
                issued += 1
        pos = 0
        for i, (src, fn, hold) in enumerate(steps):
            if src is not None:
                issue_upto(pos + NRING - hold)
                fn(slot_of[i])
                pos += 1
            else:
                issue_upto(pos + NRING - 1)
                fn(None)

    def tiles_of(g, stage):
        return [1, 2, 3, 4] if (g == 0 and stage == 1) else [0, 1, 2, 3, 4]

    HALO0 = 96

    def tblocks(g, stage):
        if g == 0:
            return [(128, 512)] if stage == 1 else [(HALO0, 384 - HALO0), (384, 256)]
        return [(0, 384), (384, 256)]

    def hkeys(t0, n):
        return [("hT", t) for t in range(t0 // 128, (t0 + n + 127) // 128)]

    def ykeys(t0, n):
        return [("Y", t) for t in range(t0 // 128, (t0 + n + 127) // 128)]

    def fm_unit(slot, mi, act, akeys, t0, n):
        b = bank()

        def f():
            for k in range(KC):
                ins = PE.matmul(ps[:, b, 0:n], ring[:, slot, k, mi * 128:(mi + 1) * 128],
                                act[:, k, t0:t0 + n], start=(k == 0), stop=(k == KC - 1))
            return ins
        S.op("pe", f, reads=[("w", slot)] + akeys, writes=[("ps", b)])
        return b

    def tm_unit(slot, act, akey, t):
        b = bank()

        def f():
            for k in range(KC):
                ins = PE.matmul(ps[:, b, :], act[:, k, t * 128:(t + 1) * 128],
                                ring[:, slot, k, :], start=(k == 0), stop=(k == KC - 1))
            return ins
        S.op("pe", f, reads=[("w", slot), akey], writes=[("ps", b)])
        return b

    T4ALL = [("T4", r) for r in range(4)]
    RKEYS = ([("vq", t, q) for t in range(NT) for q in range(4)] + [("ci", m) for m in range(16)]
             + [("gl", m) for m in range(16)] + ["cipast", "glpast"])
    YKEYS = ([("Y", t) for t in range(NT)] + ["sgt", "cst", ("sgy", 0), ("sgy", 1)]
             + [("dg", i_) for i_ in range(22)])
    XNALL = [("xn", r) for r in range(3)]

    def vcol(v, c):
        return vfm[:, v * 16 + c: v * 16 + c + 1]

    def setup():
        S.dma("sp", ident[:], consts_in[0], writes=["ident"])
        S.dma("sp", maskP[:], consts_in[1], writes=["maskP"])
        S.dma("sp", maskS[:], consts_in[2], writes=["maskS"])
        S.dma("sp", hmask[:], hmask_in, writes=["hmask"])
        vld = xn[:, 0:768].rearrange("p (q f) -> p q f", q=6)
        S.dma("sp", vld, vecs_in.rearrange("(q r) f -> r q f", q=6), writes=XNALL)
        S.op("dve", lambda: V.memset(onesF[:], 1.0), writes=["onesF"])
        S.op("dve", lambda: V.memset(bpast[:], 0.0), writes=["bpast"])
        S.op("dve", lambda: V.memset(cpast[:], 0.0), writes=["cpast"])
        S.op("dve", lambda: V.memset(Y[:], 0.0), writes=[("Y", t) for t in range(NT)])
        S.op("dve", lambda: V.memset(R[:], 0.0), writes=RKEYS)
        for q in range(6):
            b = bank()
            S.op("pe", lambda: PE.transpose(ps[:, b, 0:128], vld[:, q, :], ident[:]),
                 reads=XNALL + ["ident"], writes=[("ps", b)])
            S.op("dve", lambda: V.tensor_copy(out=vfm[:, q * 128:(q + 1) * 128], in_=ps[:, b, 0:128]),
                 reads=[("ps", b)], writes=["vfm"])

    def load_group(g):
        for t in range(NT):
            r0 = (g * NT + t) * 128
            S.dma("sp", xt[:, t, :], xin[r0:r0 + 128, :], writes=[("x", t)])

    def dump_dbg(g):
        for t in range(NT):
            r0 = (g * NT + t) * 128
            S.dma("sp", xdbg[r0:r0 + 128, :], xt[:, t, :], reads=[("x", t)], is_output=True)

    norm_ctr = [0]

    def rms_batch(tl):
        r = norm_ctr[0] % 2
        norm_ctr[0] += 1
        sk = ("stat", r)
        for i, t in enumerate(tl):
            S.op("act", lambda: A.activation(out=hT[:, :, t * 128:(t + 1) * 128],
                                             in_=xt[:, t, :].rearrange("p (c n) -> p c n", c=KC),
                                             func=AF.Square, accum_out=stat[:, r, 0, i:i + 1]),
                 reads=[("x", t)], writes=[("hT", t), sk])
        n = len(tl)
        S.op("dve", lambda: V.tensor_scalar(out=stat[:, r, 1, 0:n], in0=stat[:, r, 0, 0:n], scalar1=1.0 / D,
                                            scalar2=EPS, op0=ALU.mult, op1=ALU.add),
             reads=[sk], writes=[sk])
        S.op("act", lambda: A.activation(out=stat[:, r, 2, 0:n], in_=stat[:, r, 1, 0:n], func=AF.Sqrt),
             reads=[sk], writes=[sk])
        S.op("dve", lambda: V.reciprocal(out=stat[:, r, 3, 0:n], in_=stat[:, r, 2, 0:n]),
             reads=[sk], writes=[sk])
        return r

    def rms_apply(t, r, i):
        S.op("act", lambda: A.activation(out=xn[:], in_=xt[:, t, :], func=AF.Copy,
                                         scale=stat[:, r, 3, i:i + 1]),
             reads=[("x", t), ("stat", r)], writes=XNALL)

    def norm_to_hT(g, stage, vrow):
        tl = tiles_of(g, stage)
        r = rms_batch(tl)
        for i, t in enumerate(tl):
            rms_apply(t, r, i)
            for q in range(4):
                b = bank()

                def f():
                    for c in range(4):
                        ins = PE.transpose(ps[:, b, c * 128:(c + 1) * 128],
                                           xn[:, (4 * q + c) * 128:(4 * q + c + 1) * 128], ident[:])
                    return ins
                S.op("pe", f, reads=XNALL + ["ident"], writes=[("ps", b)])
                gb = vfm[:, vrow * 16 + 4 * q: vrow * 16 + 4 * q + 4].unsqueeze(2).broadcast_to([128, 4, 128])
                S.op("dve", lambda: V.tensor_tensor(
                    out=hT[:, 4 * q:4 * q + 4, t * 128:(t + 1) * 128],
                    in0=ps[:, b, :].rearrange("p (c n) -> p c n", c=4), in1=gb, op=ALU.mult),
                    reads=[("ps", b), "vfm"], writes=[("hT", t)])

    def out_proj(g, stage, wname, widx, r0, bias_in_T4=False):
        for cb in range(4):
            def consume(slot, cb=cb):
                for t in tiles_of(g, stage):
                    b = tm_unit(slot, Y, ("Y", t), t)
                    S.op("dve", lambda: V.tensor_tensor(out=xt[:, t, cb * 512:(cb + 1) * 512],
                                                        in0=ps[:, b, :], in1=xt[:, t, cb * 512:(cb + 1) * 512],
                                                        op=ALU.add),
                         reads=[("ps", b), ("x", t)], writes=[("x", t)])
                    if bias_in_T4:
                        S.op("dve", lambda: V.tensor_tensor(out=xt[:, t, cb * 512:(cb + 1) * 512],
                                                            in0=xt[:, t, cb * 512:(cb + 1) * 512],
                                                            in1=T4[:, cb, 0:512], op=ALU.add),
                             reads=[("x", t)] + T4ALL, writes=[("x", t)])
            add_w(wsrc(wname, widx, r0, cb * 512), consume)

    def ffn(g, stage, l):
        def prep_f():
            S.alias_barrier(YKEYS)
            norm_to_hT(g, stage, V_NFFN + l)
        add(prep_f)
        rot = [0]
        for j in range(4):
            for cb in range(4):
                def consume(slot, cb=cb):
                    for mi in range(4):
                        for (t0, n) in tblocks(g, stage):
                            b = fm_unit(slot, mi, hT, hkeys(t0, n), t0, n)
                            r = rot[0] % 4
                            rot[0] += 1
                            S.op("act", lambda: A.activation(out=T4[:, r, 0:n], in_=ps[:, b, 0:n], func=AF.Relu),
                                 reads=[("ps", b)], writes=[("T4", r)])
                            S.op("dve", lambda: V.tensor_tensor(out=Y[:, cb * 4 + mi, t0:t0 + n],
                                                                in0=T4[:, r, 0:n], in1=T4[:, r, 0:n], op=ALU.mult),
                                 reads=[("T4", r)], writes=ykeys(t0, n))
                add_w(wsrc("ffn_w_up", l, 0, j * 2048 + cb * 512), consume)
            out_proj(g, stage, "ffn_w_down", l, j * 2048)

    def mixer_a(g, stage, l):
        j = l // 3
        v32 = R[:, 0:NT * D].rearrange("p (t f) -> p t f", t=NT)
        tl = tiles_of(g, stage)
        sample_tile = 4 if g == 1 else None
        nkind = 2 if g == 1 else 1

        def prep_a():
            S.alias_barrier(RKEYS)
            norm_to_hT(g, stage, V_NMIX + l)
        add(prep_a)

        for cb in range(4):
            def consume(slot, cb=cb):
                for t in tl:
                    b = tm_unit(slot, hT, ("hT", t), t)
                    S.op("act", lambda: A.activation(out=v32[:, t, cb * 512:(cb + 1) * 512], in_=ps[:, b, :], func=AF.Gelu),
                         reads=[("ps", b)], writes=[("vq", t, cb)])
                    S.op("dve", lambda: V.bn_stats(out=bnst[:, t, cb, :], in_=v32[:, t, cb * 512:(cb + 1) * 512]),
                         reads=[("vq", t, cb)], writes=[("bn", t)])
            add_w(wsrc("a_w_in", j, 0, D + cb * 512), consume)

        def ln_quarter_load(q):
            cs = slice(q * 512, (q + 1) * 512)
            i_ = 0
            S.dma("sp", lnp[:, i_, 0, :], W["a_ln_g"][j:j + 1, cs].broadcast_to([128, 512]), writes=[("lnp", i_)])
            S.dma("sp", lnp[:, i_, 1, :], W["a_ln_b"][j:j + 1, cs].broadcast_to([128, 512]), writes=[("lnp", i_)])

        def ln_quarter_tile(q, t):
            cs = slice(q * 512, (q + 1) * 512)
            i_ = 0
            S.op("act", lambda: A.activation(out=v32[:, t, cs], in_=v32[:, t, cs], func=AF.Identity,
                                             bias=mv[:, t, 2:3], scale=mv[:, t, 1:2]),
                 reads=[("vq", t, q), "mv"], writes=[("vq", t, q)])
            S.op("dve", lambda: V.tensor_tensor(out=v32[:, t, cs], in0=v32[:, t, cs], in1=lnp[:, i_, 0, :], op=ALU.mult),
                 reads=[("vq", t, q), ("lnp", i_)], writes=[("vq", t, q)])
            S.op("dve", lambda: V.tensor_tensor(out=v32[:, t, cs], in0=v32[:, t, cs], in1=lnp[:, i_, 1, :], op=ALU.add),
                 reads=[("vq", t, q), ("lnp", i_)], writes=[("vq", t, q)])
            if q == 3 and t == tl[-1] and sample_tile is not None:
                S.dma("sp", av_out[j], v32[:, sample_tile, :],
                      reads=[("vq", sample_tile, q_) for q_ in range(4)], is_output=True)

        def ln_quarter(q):
            ln_quarter_load(q)
            for t in tl:
                ln_quarter_tile(q, t)

        def ln_v():
            prep_group(0)
            for t in tl:
                S.op("dve", lambda: V.bn_aggr(out=mv[:, t, 0:2], in_=bnst[:, t].rearrange("p a b -> p (a b)")),
                     reads=[("bn", t)], writes=["mv"])
            t0_, t1_ = tl[0], tl[-1] + 1
            S.op("act", lambda: A.activation(out=mv[:, t0_:t1_, 1:2], in_=mv[:, t0_:t1_, 1:2], func=AF.Sqrt, bias=EPS),
                 reads=["mv"], writes=["mv"])
            S.op("dve", lambda: V.reciprocal(out=mv[:, t0_:t1_, 1:2], in_=mv[:, t0_:t1_, 1:2]),
                 reads=["mv"], writes=["mv"])
            S.op("dve", lambda: V.scalar_tensor_tensor(out=mv[:, t0_:t1_, 2:3], in0=mv[:, t0_:t1_, 0:1], scalar=-1.0,
                                                       in1=mv[:, t0_:t1_, 1:2], op0=ALU.mult, op1=ALU.mult),
                 reads=["mv"], writes=["mv"])
            ln_quarter(0)
        add(ln_v)

        def prep_group(grp):
            i = grp % 2
            for kind in range(nkind):
                if kind == 0:
                    S.dma("sp", wld[:, kind, :], W["a_w_s"][j, grp], writes=[("wld", kind)])
                    S.op("dve", lambda: V.tensor_tensor(out=wld[:, kind, :], in0=wld[:, kind, :], in1=maskP[:], op=ALU.mult),
                         reads=[("wld", kind), "maskP"], writes=[("wld", kind)])
                    S.dma("sp", bsb[:, i, 0, :], W["a_b_s"][j, grp:grp + 1, :].broadcast_to([128, 128]), writes=[("bsr", i)])
                else:
                    S.dma("sp", wrep[:, 0, :], W["a_ws8"][j, grp], writes=["wrep"])
                    S.op("dve", lambda: V.tensor_tensor(
                        out=wld[:, kind, :].rearrange("p (a b) -> p a b", a=16),
                        in0=wrep[:, 0, :].unsqueeze(1).broadcast_to([128, 16, 8]),
                        in1=maskS[:].rearrange("p (a b) -> p a b", a=16), op=ALU.mult),
                        reads=["wrep", "maskS"], writes=[("wld", kind)])
                    S.dma("sp", bsb[:, i, 1, :], W["a_bs8"][j, grp:grp + 1, :].broadcast_to([128, 128]), writes=[("bsr", i)])
                b = bank()
                S.op("pe", lambda: PE.transpose(ps[:, b, 0:128], wld[:, kind, :], ident[:]),
                     reads=[("wld", kind), "ident"], writes=[("ps", b)])
                S.op("act", lambda: A.activation(out=wsTb[:, i, kind, :], in_=ps[:, b, 0:128], func=AF.Copy),
                     reads=[("ps", b)], writes=[("wsT", i)])

        for cb in range(4):
            def consume(slot, cb=cb):
                for mi in range(4):
                    m = cb * 4 + mi
                    grp = m // 2
                    i = grp % 2
                    if mi == 0:
                        lnq = list(tl) if cb + 1 < 4 else []
                        if lnq:
                            ln_quarter_load(cb + 1)
                    if m % 2 == 0 and grp + 1 < 8:
                        prep_group(grp + 1)
                    for (t0, n) in tblocks(g, stage):
                        bu = fm_unit(slot, mi, hT, hkeys(t0, n), t0, n)
                        bg = bank()
                        trange = list(range(t0 // 128, (t0 + n + 127) // 128))

                        def gate():
                            for t in trange:
                                kd = 1 if t == sample_tile else 0
                                lo, hi = max(t * 128, t0), min((t + 1) * 128, t0 + n)
                                o = ps[:, bg, lo - t0:hi - t0]
                                ins = PE.matmul(o, v32[:, t, m * 128:(m + 1) * 128], wsTb[:, i, kd, lo - t * 128:hi - t * 128],
                                                start=True, stop=True)
                            return ins
                        S.op("pe", gate, reads=[("vq", t, cb) for t in trange] + [("wsT", i)],
                             writes=[("ps", bg)])
                        segs_ = []
                        for t in trange:
                            kd = 1 if t == sample_tile else 0
                            lo, hi = max(t * 128, t0), min((t + 1) * 128, t0 + n)
                            full = (kd == 0 and hi - lo == 128)
                            if full and segs_ and segs_[-1][0] == "full":
                                segs_[-1][2] += 1
                            else:
                                segs_.append(["full" if full else "part", lo, 1, hi, kd, t])
                        for kind_, lo, cnt_, hi, kd, t in segs_:
                            if kind_ == "full":
                                o_ = ps[:, bg, lo - t0:lo - t0 + cnt_ * 128].rearrange("p (a c) -> p a c", c=128)
                                i1_ = bsb[:, i, 0, :].unsqueeze(1).broadcast_to([128, cnt_, 128])
                            else:
                                o_ = ps[:, bg, lo - t0:hi - t0]
                                i1_ = bsb[:, i, kd, lo - t * 128:hi - t * 128]
                            S.op("dve", lambda: V.tensor_tensor(out=o_, in0=o_, in1=i1_, op=ALU.add),
                                 reads=[("ps", bg), ("bsr", i)], writes=[("ps", bg)])
                        r = (m * 2 + (0 if t0 < 384 else 1)) % 4
                        S.op("act", lambda: A.activation(out=T4[:, r, 0:n], in_=ps[:, bu, 0:n], func=AF.Gelu),
                             reads=[("ps", bu)], writes=[("T4", r)])
                        S.op("dve", lambda: V.tensor_tensor(out=Y[:, m, t0:t0 + n], in0=ps[:, bg, 0:n],
                                                            in1=T4[:, r, 0:n], op=ALU.mult),
                             reads=[("ps", bg), ("T4", r)], writes=ykeys(t0, n))
                        if cb + 1 < 4 and lnq:
                            ln_quarter_tile(cb + 1, lnq.pop(0))
                while cb + 1 < 4 and lnq:
                    ln_quarter_tile(cb + 1, lnq.pop(0))
            add_w(wsrc("a_w_in", j, 0, cb * 512), consume)
        out_proj(g, stage, "a_w_out", j, 0)

    def mixer_b(g, l):
        stage = 0
        NP = 640 if g == 0 else 512
        ci_p = R[:, 0:16 * (NP + 2)].rearrange("p (c w) -> p c w", c=16)
        ci_s = None
        if g == 1:
            o0 = 16 * (NP + 2)
            ci_s = R[:, o0:o0 + 16 * 160].rearrange("p (c s w) -> p c s w", c=16, s=16)

        def segs(t0, n):
            if g == 0 or t0 + n <= 512:
                return [(0, n, False)]
            return [(0, 512 - t0, False), (512 - t0, 128, True)]

        def prep():
            S.alias_barrier(RKEYS)
            norm_to_hT(g, stage, V_NMIX + l)
            S.op("dve", lambda: V.tensor_copy(out=ci_p[:, :, 0:2], in_=bpast[:]), reads=["bpast"], writes=["cipast"])
            if g == 1:
                S.dma("sp", xn[0:32, :], sb_in, writes=XNALL)
                for q in range(4):
                    b = bank()

                    def f():
                        for c in range(4):
                            ins = PE.transpose(ps[:, b, c * 32:(c + 1) * 32],
                                               xn[0:32, (4 * q + c) * 128:(4 * q + c + 1) * 128], ident[0:32, 0:32])
                        return ins
                    S.op("pe", f, reads=XNALL + ["ident"], writes=[("ps", b)])
                    S.op("dve", lambda: V.tensor_copy(
                        out=ci_s[:, 4 * q:4 * q + 4, :, 0:2],
                        in_=ps[:, b, 0:128].rearrange("p (c s w) -> p c s w", c=4, s=16)),
                        reads=[("ps", b)], writes=["cipast"])
        add(prep)

        for cb in range(4):
            def c_hx(slot, cb=cb):
                for mi in range(4):
                    for (t0, n) in tblocks(g, stage):
                        b = fm_unit(slot, mi, hT, hkeys(t0, n), t0, n)
                        S.op("act", lambda: A.activation(out=T4[:, mi, t0:t0 + n], in_=ps[:, b, 0:n], func=AF.Copy),
                             reads=[("ps", b)], writes=[("T4", mi)])
            add_w(wsrc("b_w_in", 0, 0, 2 * D + cb * 512), c_hx)

            def c_c(slot, cb=cb):
                for mi in range(4):
                    m = cb * 4 + mi
                    for (t0, n) in tblocks(g, stage):
                        b = fm_unit(slot, mi, hT, hkeys(t0, n), t0, n)
                        for (o, ln, smp) in segs(t0, n):
                            if not smp:
                                S.op("dve", lambda: V.tensor_tensor(out=ci_p[:, m, 2 + t0 + o:2 + t0 + o + ln],
                                                                    in0=ps[:, b, o:o + ln], in1=T4[:, mi, t0 + o:t0 + o + ln],
                                                                    op=ALU.mult),
                                     reads=[("ps", b), ("T4", mi)], writes=[("ci", m)])
                            else:
                                S.op("dve", lambda: V.tensor_tensor(
                                    out=ci_s[:, m, :, 2:10],
                                    in0=ps[:, b, o:o + ln].rearrange("p (s w) -> p s w", s=16),
                                    in1=T4[:, mi, t0 + o:t0 + o + ln].rearrange("p (s w) -> p s w", s=16), op=ALU.mult),
                                    reads=[("ps", b), ("T4", mi)], writes=[("ci", m)])
                for k in range(3):
                    for smp in ((False, True) if g == 1 else (False,)):
                        for mi in range(4):
                            m = cb * 4 + mi
                            rk = [("ci", m), "cipast", "vfm"]
                            if smp:
                                acc = T4[:, mi, 512:640].rearrange("p (s w) -> p s w", s=16)
                                src = ci_s[:, m, :, k:k + 8]
                            else:
                                acc = T4[:, mi, 0:NP]
                                src = ci_p[:, m, k:k + NP]
                            if k == 0:
                                S.op("dve", lambda: V.tensor_scalar(out=acc, in0=src, scalar1=vcol(V_BCW, m),
                                                                    scalar2=None, op0=ALU.mult),
                                     reads=rk + [("T4", mi)], writes=[("T4", mi)])
                            else:
                                S.op("dve", lambda: V.scalar_tensor_tensor(out=acc, in0=src, scalar=vcol(V_BCW + k, m),
                                                                           in1=acc, op0=ALU.mult, op1=ALU.add),
                                     reads=rk + [("T4", mi)], writes=[("T4", mi)])
            add_w(wsrc("b_w_in", 0, 0, D + cb * 512), c_c)

            def c_b(slot, cb=cb):
                for mi in range(4):
                    m = cb * 4 + mi
                    for (t0, n) in tblocks(g, stage):
                        b = fm_unit(slot, mi, hT, hkeys(t0, n), t0, n)
                        S.op("dve", lambda: V.tensor_tensor(out=Y[:, m, t0:t0 + n], in0=ps[:, b, 0:n],
                                                            in1=T4[:, mi, t0:t0 + n], op=ALU.mult),
                             reads=[("ps", b), ("T4", mi)], writes=ykeys(t0, n))
            add_w(wsrc("b_w_in", 0, 0, cb * 512), c_b)

        def tails():
            allci = [("ci", m) for m in range(16)]
            S.op("dve", lambda: V.tensor_copy(out=bpast[:], in_=ci_p[:, :, NP:NP + 2]), reads=allci, writes=["bpast"])
            if g == 1:
                for q in range(4):
                    b = bank()

                    def f():
                        for c in range(4):
                            ins = PE.transpose(ps[0:2, b, c * 128:(c + 1) * 128], bpast[:, 4 * q + c, :], ident[:])
                        return ins
                    S.op("pe", f, reads=["bpast", "ident"], writes=[("ps", b)])
                    S.op("act", lambda: A.activation(out=xn[0:2, q * 512:(q + 1) * 512], in_=ps[0:2, b, :], func=AF.Copy),
                         reads=[("ps", b)], writes=XNALL)
                S.dma("sp", bp_out, xn[0:2, :], reads=XNALL, is_output=True)
                tl32 = T4[:, 0, 0:512].rearrange("p (c s w) -> p c s w", c=16, s=16)
                S.op("dve", lambda: V.tensor_copy(out=tl32, in_=ci_s[:, :, :, 8:10]), reads=allci, writes=[("T4", 0)])
                for q in range(4):
                    b = bank()

                    def f2():
                        for c in range(4):
                            ins = PE.transpose(ps[0:32, b, c * 128:(c + 1) * 128],
                                               T4[:, 0, (4 * q + c) * 32:(4 * q + c + 1) * 32], ident[:])
                        return ins
                    S.op("pe", f2, reads=[("T4", 0), "ident"], writes=[("ps", b)])
                    S.op("act", lambda: A.activation(out=xn[0:32, q * 512:(q + 1) * 512], in_=ps[0:32, b, :], func=AF.Copy),
                         reads=[("ps", b)], writes=XNALL)
                S.dma("sp", bs_out, xn[0:32, :], reads=XNALL, is_output=True)
        add(tails)
        out_proj(g, stage, "b_w_out", 0, 0)

    def mixer_c(g, l):
        GL = R[:, 0:16 * 670].rearrange("p (c w) -> p c w", c=16)
        NP = 640 if g == 0 else 512
        C0 = 128 if g == 0 else 0
        stl = xn[0:120, 0:512].rearrange("p (q f) -> p q f", q=4)
        Yf = Y[:].rearrange("p a b -> p (a b)").bitcast(F32)
        sgs = [xn[:, 512:1120].rearrange("p (s w) -> p s w", s=16),
               xn[:, 1120:1728].rearrange("p (s w) -> p s w", s=16),
               Yf[:, 3840:4448].rearrange("p (s w) -> p s w", s=16),
               Yf[:, 4448:5056].rearrange("p (s w) -> p s w", s=16)]
        sgkeys = [("xn", 1), ("xn", 2), ("sgy", 0), ("sgy", 1)]
        sgt = Yf[:, 0:480]
        cst = Yf[0:120, 512:1024].rearrange("p (q f) -> p q f", q=4)
        NDG = 22
        dg = Yf[:, 1024:1024 + NDG * 128].rearrange("p (s f) -> p s f", s=NDG)
        dg_ctr = [0]
        KPE = 11
        sbanks = []

        def segs(t0, n):
            if g == 0 or t0 + n <= 512:
                return [(0, n, False)]
            return [(0, 512 - t0, False), (512 - t0, 128, True)]

        def prep():
            S.alias_barrier(RKEYS)
            S.alias_barrier(YKEYS)
            norm_to_hT(g, 0, V_NMIX + l)
            S.op("dve", lambda: V.tensor_copy(out=GL[:, :, 0:30], in_=cpast[:]), reads=["cpast"], writes=["glpast"])
            for i in range(4):
                b = bank()
                bank_state["reserved"].add(b)
                sbanks.append(b)
        add(prep)

        wg_slot = {}
        deferred = []
        for cb in range(4):
            add_w(wsrc("c_w_pw1", 0, 0, D + cb * 512), lambda slot, cb=cb: wg_slot.__setitem__(cb, slot))

            def c_a(slot, cb=cb):
                n_out = 512
                gslot = wg_slot[cb]
                for p in range(2):
                    base = 2 * ((cb * 2 + p) % 2)
                    pair = (2 * p, 2 * p + 1)

                    def sl(mi):
                        return base + (mi % 2)
                    for mi in pair:
                        m = cb * 4 + mi
                        for (t0, n) in tblocks(g, 0):
                            b = fm_unit(gslot, mi, hT, hkeys(t0, n), t0, n)
                            S.op("act", lambda: A.activation(out=T4[:, sl(mi), t0:t0 + n], in_=ps[:, b, 0:n], func=AF.Sigmoid,
                                                             bias=vcol(V_CB1 + 1, m)),
                                 reads=[("ps", b), "vfm"], writes=[("T4", sl(mi))])
                    for mi in pair:
                        m = cb * 4 + mi
                        ts = sl(mi)
                        sg, sgk = sgs[ts], sgkeys[ts]
                        for (t0, n) in tblocks(g, 0):
                            b = fm_unit(slot, mi, hT, hkeys(t0, n), t0, n)
                            for (o, ln, smp) in segs(t0, n):
                                if not smp:
                                    S.op("dve", lambda: V.scalar_tensor_tensor(
                                        out=GL[:, m, 30 + t0 + o:30 + t0 + o + ln], in0=ps[:, b, o:o + ln],
                                        scalar=vcol(V_CB1, m), in1=T4[:, ts, t0 + o:t0 + o + ln],
                                        op0=ALU.add, op1=ALU.mult),
                                        reads=[("ps", b), ("T4", ts), "vfm"], writes=[("gl", m)])
                                else:
                                    S.op("dve", lambda: V.scalar_tensor_tensor(
                                        out=sg[:, :, 30:38], in0=ps[:, b, o:o + ln].rearrange("p (s w) -> p s w", s=16),
                                        scalar=vcol(V_CB1, m), in1=T4[:, ts, t0 + o:t0 + o + ln].rearrange("p (s w) -> p s w", s=16),
                                        op0=ALU.add, op1=ALU.mult),
                                        reads=[("ps", b), ("T4", ts), "vfm"], writes=[sgk])
                        if g == 0:
                            S.op("dve", lambda: V.tensor_scalar(out=GL[:, m, 30:158], in0=GL[:, m, 30:158],
                                                                scalar1=hmask[:, 0:1], scalar2=None, op0=ALU.mult),
                                 reads=[("gl", m), "hmask"], writes=[("gl", m)])
                        else:
                            S.dma("sp", stl, sc_in.rearrange("(q r) f -> r q f", q=4)[:, :, m * 128:(m + 1) * 128],
                                  writes=[("xn", 0)])
                            b = bank()

                            def f():
                                for q in range(4):
                                    ins = PE.transpose(ps[:, b, q * 120:(q + 1) * 120], stl[:, q, :], ident[0:120, 0:120])
                                return ins
                            S.op("pe", f, reads=[("xn", 0), "ident"], writes=[("ps", b)])
                            S.op("act", lambda: A.activation(out=sg[:, :, 0:30],
                                                             in_=ps[:, b, 0:480].rearrange("p (s r) -> p s r", s=16), func=AF.Copy),
                                 reads=[("ps", b)], writes=[sgk])
                    cbanks = {}
                    for mi in pair:
                        m = cb * 4 + mi
                        bc = bank()
                        cbanks[mi] = bc
                        taps = list(range(31 - KPE, 31))
                        sls = [(dg_ctr[0] + i_) % NDG for i_ in range(KPE)]
                        dg_ctr[0] += KPE

                        def build():
                            for k, sl_ in zip(taps, sls):
                                ins = A.activation(out=dg[:, sl_, :], in_=ident[:], func=AF.Copy, scale=vcol(V_CCW + k, m))
                            return ins
                        S.op("act", build, reads=["ident", "vfm"], writes=[("dg", sl_) for sl_ in sls])

                        def ptaps():
                            for k, sl_ in zip(taps, sls):
                                ins = PE.matmul(ps[:, bc, :], dg[:, sl_, :], GL[:, m, C0 + k:C0 + k + n_out],
                                                start=(k == 31 - KPE), stop=(k == 30))
                            return ins
                        S.op("pe", ptaps, reads=[("dg", sl_) for sl_ in sls] + [("gl", m), "glpast"], writes=[("ps", bc)])
                    for fn_ in deferred:
                        fn_()
                    deferred.clear()
                    kadd = min(12, 30 - KPE)
                    for k in range(31):
                        for smp in ((False, True) if g == 1 else (False,)):
                            if (not smp) and k >= 31 - KPE:
                                continue
                            for mi in pair:
                                m = cb * 4 + mi
                                ts = sl(mi)
                                if smp:
                                    acc = T4[:, ts, 512:640].rearrange("p (s w) -> p s w", s=16)
                                    src = sgs[ts][:, :, k:k + 8]
                                    rk = [sgkeys[ts], "vfm", ("T4", ts)]
                                else:
                                    acc = T4[:, ts, C0:C0 + n_out]
                                    src = GL[:, m, C0 + k:C0 + k + n_out]
                                    rk = [("gl", m), "glpast", "vfm", ("T4", ts)]
                                if k == 0:
                                    S.op("dve", lambda: V.tensor_scalar(out=acc, in0=src, scalar1=vcol(V_CCW, m),
                                                                        scalar2=None, op0=ALU.mult),
                                         reads=rk, writes=[("T4", ts)])
                                else:
                                    S.op("dve", lambda: V.scalar_tensor_tensor(out=acc, in0=src, scalar=vcol(V_CCW + k, m),
                                                                               in1=acc, op0=ALU.mult, op1=ALU.add),
                                         reads=rk, writes=[("T4", ts)])
                        if k == kadd:
                            for mi in pair:
                                ts = sl(mi)
                                acc = T4[:, ts, C0:C0 + n_out]
                                S.op("dve", lambda: V.tensor_tensor(out=acc, in0=ps[:, cbanks[mi], :], in1=acc, op=ALU.add),
                                     reads=[("ps", cbanks[mi]), ("T4", ts)], writes=[("T4", ts)])
                    for mi in pair:
                        m = cb * 4 + mi
                        ts = sl(mi)
                        sg, sgk = sgs[ts], sgkeys[ts]
                        if g == 1:
                            def tail(m=m, sg=sg, sgk=sgk):
                                b2 = bank()
                                S.op("act", lambda: A.activation(out=sgt[:, 0:480].rearrange("p (s r) -> p s r", s=16),
                                                                 in_=sg[:, :, 8:38], func=AF.Copy),
                                     reads=[sgk], writes=["sgt"])

                                def f3():
                                    for q in range(4):
                                        ins = PE.transpose(ps[0:120, b2, q * 128:(q + 1) * 128], sgt[:, q * 120:(q + 1) * 120], ident[:])
                                    return ins
                                S.op("pe", f3, reads=["sgt", "ident"], writes=[("ps", b2)])
                                S.op("act", lambda: A.activation(out=cst, in_=ps[0:120, b2, :].rearrange("p (q f) -> p q f", q=4),
                                                                 func=AF.Copy),
                                     reads=[("ps", b2)], writes=["cst"])
                                S.dma("sp", cs_out.rearrange("(q r) f -> r q f", q=4)[:, :, m * 128:(m + 1) * 128], cst,
                                      reads=["cst"], is_output=True)
                            deferred.append(tail)
                        S.op("dve", lambda: V.tensor_copy(out=cpast[:, m, :], in_=GL[:, m, NP:NP + 30]),
                             reads=[("gl", m), "glpast"], writes=["cpast"])
                        S.op("dve", lambda: V.tensor_scalar(out=GL[:, m, C0:640], in0=T4[:, ts, C0:640],
                                                            scalar1=vcol(V_CCB, m), scalar2=None, op0=ALU.add),
                             reads=[("T4", ts), "vfm", "cpast"], writes=[("gl", m)])
                        S.op("act", lambda: A.activation(out=T4[:, ts, C0:640], in_=GL[:, m, C0:640], func=AF.Square),
                             reads=[("gl", m)], writes=[("T4", ts)])
                        def stats(m=m, ts=ts):
                            for ib, (t0, n) in enumerate(tblocks(g, 1)):
                                def f4():
                                    PE.matmul(ps[:, sbanks[ib], 0:n], onesF[:], GL[:, m, t0:t0 + n], start=(m == 0), stop=(m == 15))
                                    return PE.matmul(ps[:, sbanks[2 + ib], 0:n], onesF[:], T4[:, ts, t0:t0 + n],
                                                     start=(m == 0), stop=(m == 15))
                                S.op("pe", f4, reads=[("gl", m), ("T4", ts), "onesF"],
                                     writes=[("ps", sbanks[ib]), ("ps", sbanks[2 + ib])])
                        deferred.append(stats)
            add_w(wsrc("c_w_pw1", 0, 0, cb * 512), c_a, hold=1)

        def ln_c():
            for fn_ in deferred:
                fn_()
            deferred.clear()
            for ib, (t0, n) in enumerate(tblocks(g, 1)):
                sl = slice(t0, t0 + n)
                b1, b2 = sbanks[ib], sbanks[2 + ib]
                S.op("act", lambda: A.activation(out=T4[:, 0, sl], in_=ps[:, b1, 0:n], func=AF.Copy, scale=1.0 / D),
                     reads=[("ps", b1)], writes=[("T4", 0)])
                S.op("dve", lambda: V.tensor_tensor(out=T4[:, 1, sl], in0=T4[:, 0, sl], in1=T4[:, 0, sl], op=ALU.mult),
                     reads=[("T4", 0)], writes=[("T4", 1)])
                S.op("dve", lambda: V.scalar_tensor_tensor(out=T4[:, 2, sl], in0=ps[:, b2, 0:n], scalar=1.0 / D,
                                                           in1=T4[:, 1, sl], op0=ALU.mult, op1=ALU.subtract),
                     reads=[("ps", b2), ("T4", 1)], writes=[("T4", 2)])
                S.op("act", lambda: A.activation(out=T4[:, 2, sl], in_=T4[:, 2, sl], func=AF.Sqrt, bias=EPS),
                     reads=[("T4", 2)], writes=[("T4", 2)])
                S.op("dve", lambda: V.reciprocal(out=T4[:, 2, sl], in_=T4[:, 2, sl]),
                     reads=[("T4", 2)], writes=[("T4", 2)])
            for b in sbanks:
                bank_state["reserved"].discard(b)
            for m in range(16):
                S.op("dve", lambda: V.tensor_tensor(out=GL[:, m, C0:640], in0=GL[:, m, C0:640], in1=T4[:, 0, C0:640],
                                                    op=ALU.subtract),
                     reads=[("gl", m), ("T4", 0)], writes=[("gl", m)])
                S.op("dve", lambda: V.tensor_tensor(out=GL[:, m, C0:640], in0=GL[:, m, C0:640], in1=T4[:, 2, C0:640],
                                                    op=ALU.mult),
                     reads=[("gl", m), ("T4", 2)], writes=[("gl", m)])
                S.op("act", lambda: A.activation(out=Y[:, m, C0:640], in_=GL[:, m, C0:640], func=AF.Silu,
                                                 bias=vcol(V_CLB, m), scale=vcol(V_CLG, m)),
                     reads=[("gl", m), "vfm"], writes=ykeys(C0, 640 - C0) + ["sgt", "cst", ("sgy", 0), ("sgy", 1)] + [("dg", i_) for i_ in range(22)])
            if g == 1:
                for q in range(4):
                    b = bank()

                    def f():
                        for c in range(4):
                            ins = PE.transpose(ps[0:30, b, c * 128:(c + 1) * 128], cpast[:, 4 * q + c, :], ident[:])
                        return ins
                    S.op("pe", f, reads=["cpast", "ident"], writes=[("ps", b)])
                    S.op("act", lambda: A.activation(out=xn[0:30, q * 512:(q + 1) * 512], in_=ps[0:30, b, :], func=AF.Copy),
                         reads=[("ps", b)], writes=XNALL)
                S.dma("sp", cp_out, xn[0:30, :], reads=XNALL, is_output=True)
            for q in range(4):
                S.dma("sp", T4[:, q, 0:512], W["c_b_pw2"][0:1, q * 512:(q + 1) * 512].broadcast_to([128, 512]),
                      writes=[("T4", q)])
        add(ln_c)
        out_proj(g, 1, "c_w_pw2", 0, 0, bias_in_T4=True)

    def final_out(g):
        rst = R[:, 0:NT * D].rearrange("p (t f) -> p t f", t=NT)
        rkeys_all = ([("ci", m) for m in range(16)] + [("gl", m) for m in range(16)] + ["cipast", "glpast"])

        def f():
            S.alias_barrier(RKEYS)
            for q in range(4):
                S.dma("sp", T4[:, q, 0:512], fin_g[0:1, q * 512:(q + 1) * 512].broadcast_to([128, 512]), writes=T4ALL)
            tl = tiles_of(g, 1)
            r = rms_batch(tl)
            for i, t in enumerate(tl):
                rms_apply(t, r, i)
                rk = [("vq", t, q_) for q_ in range(4)]
                S.op("dve", lambda: V.tensor_tensor(out=rst[:, t, :].rearrange("p (a b) -> p a b", a=4),
                                                    in0=xn[:].rearrange("p (a b) -> p a b", a=4),
                                                    in1=T4[:, :, 0:512], op=ALU.mult),
                     reads=XNALL + T4ALL, writes=rk + (rkeys_all if i == 0 else []))
                r0 = (g * NT + t - 1) * 128
                S.dma("sp", y_out[r0:r0 + 128, :], rst[:, t, :], reads=rk, is_output=True)
        add(f)

    add(setup)
    for g in groups:
        add(lambda g=g: load_group(g))
        for l in layers:
            kind = l % 3
            stage = 0 if l < 2 else (1 if l > 2 else 0)
            if kind == 0:
                mixer_a(g, stage, l)
            elif kind == 1:
                mixer_b(g, l)
            else:
                mixer_c(g, l)
            ffn(g, 1 if l >= 2 else 0, l)
        if dbg:
            add(lambda g=g: dump_dbg(g))
        if final:
            final_out(g)
    run_steps()
    S.finish()
    es.close()
    return nc


def make_consts():
    t = np.arange(128)
    ident = np.eye(128, dtype=np.float32)
    tril = (t[None, :] <= t[:, None]).astype(np.float32)
    blk = ((t[:, None] // 8 == t[None, :] // 8) & (t[None, :] % 8 <= t[:, None] % 8)).astype(np.float32)
    return np.stack([ident, tril, blk]).astype(np.float32)


def make_core_inputs(inp, c):
    seq, half = c // 2, c % 2
    xp = inp["x_prompt"][seq]
    main = xp[half * 1024:(half + 1) * 1024]
    halo = xp[896:1024] if half == 1 else np.zeros((128, D), np.float32)
    smp = inp["x_sample"][c * 16:(c + 1) * 16].reshape(128, D)
    xin = np.concatenate([halo, main, smp], axis=0).astype(np.float32)
    vecs = np.concatenate([
        inp["norm_mix_g"], inp["norm_ffn_g"], inp["b_conv_w"][0], inp["c_b_pw1"][0].reshape(2, D),
        inp["c_conv_w"][0], inp["c_conv_b"], inp["c_ln_g"], inp["c_ln_b"],
        np.zeros((1, D), np.float32)], axis=0).astype(np.float32)
    assert vecs.shape[0] == NVEC
    m = dict(
        xin=np.ascontiguousarray(xin),
        sb_in=np.ascontiguousarray(inp["state_b_conv"][0, c * 16:(c + 1) * 16].reshape(32, D)),
        sc_in=np.ascontiguousarray(inp["state_c_conv"][0, c * 16:(c + 1) * 16].reshape(480, D)),
        hmask=np.full((128, 1), float(half), np.float32),
        vecs=np.ascontiguousarray(vecs.reshape(NVEC * 16, 128)),
        consts=make_consts(),
        final_norm_g=np.ascontiguousarray(inp["final_norm_g"].reshape(1, D)),
    )
    return m


def derived_inputs(inp):
    return dict(a_ws8=np.ascontiguousarray(np.tile(inp["a_w_s"][:, :, 0:8, 0:8], (1, 1, 16, 1))),
                a_bs8=np.ascontiguousarray(np.tile(inp["a_b_s"][:, :, 0:8], (1, 1, 16))))


WEIGHT_NAMES = ["a_ws8", "a_bs8", "a_w_in", "a_ln_g", "a_ln_b", "a_w_s", "a_b_s", "a_w_out", "b_w_in", "b_w_out",
                "c_w_pw1", "c_w_pw2", "c_b_pw2", "ffn_w_up", "ffn_w_down"]


def kernel(**inp):
    inp = {k: np.asarray(v) for k, v in inp.items()}
    nc = build()
    inp.update(derived_inputs(inp))
    in_maps = []
    for c in range(NCORES):
        m = make_core_inputs(inp, c)
        for k in WEIGHT_NAMES:
            m[k] = np.ascontiguousarray(inp[k], dtype=np.float32)
        in_maps.append(m)
    res = run_bass_kernel_spmd(nc, in_maps, core_ids=list(range(NCORES)))
    r = res.results
    y_prompt = np.zeros((4, 2048, D), np.float32)
    y_sample = np.zeros((128, 8, D), np.float32)
    new_a_v = np.zeros((2, 128, 8, D), np.float32)
    nb_p = np.zeros((1, 4, 2, D), np.float32)
    nb_s = np.zeros((1, 128, 2, D), np.float32)
    nc_p = np.zeros((1, 4, 30, D), np.float32)
    nc_s = np.zeros((1, 128, 30, D), np.float32)
    for c in range(NCORES):
        seq, half = c // 2, c % 2
        y = r[c]["y_out"]
        y_prompt[seq, half * 1024:(half + 1) * 1024] = y[0:1024]
        y_sample[c * 16:(c + 1) * 16] = y[1024:1152].reshape(16, 8, D)
        new_a_v[:, c * 16:(c + 1) * 16] = r[c]["av_out"].reshape(2, 16, 8, D)
        nb_s[0, c * 16:(c + 1) * 16] = r[c]["bs_out"].reshape(16, 2, D)
        nc_s[0, c * 16:(c + 1) * 16] = r[c]["cs_out"].reshape(16, 30, D)
        if half == 1:
            nb_p[0, seq] = r[c]["bp_out"]
            nc_p[0, seq] = r[c]["cp_out"]
    return (y_prompt, y_sample, new_a_v, nb_p, nb_s, nc_p, nc_s)
```
